# Optimizing a Trainium2 kernel written in Bass

```python
import jax, jax.numpy as jnp
from jax import lax
import numpy as np

D_MODEL = 2048
BATCH = 16
SEQ = 2048
DEPTH = 2

N_EVEN = (DEPTH + 1) // 2
N_ODD = DEPTH // 2
D_FF = 5632
EPS = 1e-6
V_HEAD = 128
MLA_HEADS = (D_MODEL // 2) // V_HEAD
QK_NOPE = 128
QK_ROPE = 64
Q_LORA = 512
KV_LORA = 512
ROPE_THETA = 10000.0
Q_BLOCK = 128
CONV_CH = D_MODEL - MLA_HEADS * V_HEAD
CONV_GROUPS = 8
CONV_WIDTH = 31
GM_WIDTH = D_MODEL
GM_GROUPS = 8
CHUNK = 128
OFF_KV = Q_LORA
OFF_KR = Q_LORA + KV_LORA
OFF_CONV = Q_LORA + KV_LORA + QK_ROPE
IN_EVEN = OFF_CONV + 2 * CONV_CH

kernel_name = "hybrid_mla_conv_gmlp_macaron"


def rmsnorm(x, g):
    xf = x.astype(jnp.float32)
    y = xf * lax.rsqrt(jnp.mean(xf * xf, axis=-1, keepdims=True) + EPS)
    return (y * g.astype(jnp.float32)).astype(x.dtype)


def layernorm(x, g, b):
    xf = x.astype(jnp.float32)
    mu = jnp.mean(xf, axis=-1, keepdims=True)
    var = jnp.mean(jnp.square(xf - mu), axis=-1, keepdims=True)
    y = (xf - mu) * lax.rsqrt(var + EPS)
    return (y * g.astype(jnp.float32) + b.astype(jnp.float32)).astype(x.dtype)


def swiglu(x, w_gate, w_up, w_down):
    return (jax.nn.silu(x @ w_gate) * (x @ w_up)) @ w_down


def apply_rope(x, cos, sin):
    half = x.shape[-1] // 2
    x1, x2 = x[..., :half], x[..., half:]
    return jnp.concatenate([x1 * cos - x2 * sin, x2 * cos + x1 * sin], axis=-1)


def mla_attention(c_q, c_kv, k_rope, cos, sin, q_norm_g, kv_norm_g, w_uq, w_ukv):
    B, S, _ = c_q.shape
    q = (rmsnorm(c_q, q_norm_g) @ w_uq).reshape(B, S, MLA_HEADS, QK_NOPE + QK_ROPE)
    q_nope = q[..., :QK_NOPE]
    q_pe = apply_rope(q[..., QK_NOPE:], cos[:, :, None, :], sin[:, :, None, :])
    kv = (rmsnorm(c_kv, kv_norm_g) @ w_ukv).reshape(B, S, MLA_HEADS, QK_NOPE + V_HEAD)
    k_nope, v = kv[..., :QK_NOPE], kv[..., QK_NOPE:]
    k_pe = apply_rope(k_rope, cos, sin)
    scale = (QK_NOPE + QK_ROPE) ** -0.5
    n_blk = S // Q_BLOCK
    qn_b = q_nope.reshape(B, n_blk, Q_BLOCK, MLA_HEADS, QK_NOPE).transpose(1, 0, 2, 3, 4)
    qp_b = q_pe.reshape(B, n_blk, Q_BLOCK, MLA_HEADS, QK_ROPE).transpose(1, 0, 2, 3, 4)
    key_idx = jnp.arange(S)

    def block(args):
        qn, qp, i = args
        s = (jnp.einsum('bqhd,bkhd->bhqk', qn, k_nope)
             + jnp.einsum('bqhr,bkr->bhqk', qp, k_pe)).astype(jnp.float32) * scale
        q_idx = i * Q_BLOCK + jnp.arange(Q_BLOCK)
        s = jnp.where(key_idx[None, :] <= q_idx[:, None], s, -jnp.inf)
        p = jax.nn.softmax(s, axis=-1).astype(v.dtype)
        return jnp.einsum('bhqk,bkhd->bqhd', p, v)

    o = lax.map(block, (qn_b, qp_b, jnp.arange(n_blk)))
    return o.transpose(1, 0, 2, 3, 4).reshape(B, S, MLA_HEADS * V_HEAD)


def conformer_conv(h, conv_w, conv_b, norm_g, norm_b):
    B, S, _ = h.shape
    a, gate = h[..., :CONV_CH], h[..., CONV_CH:]
    z = a * jax.nn.sigmoid(gate)
    z = lax.conv_general_dilated(z, conv_w[:, None, :], window_strides=(1,),
                                 padding=[(CONV_WIDTH - 1, 0)],
                                 dimension_numbers=('NWC', 'WIO', 'NWC'),
                                 feature_group_count=CONV_CH) + conv_b
    gsz = CONV_CH // CONV_GROUPS
    z = layernorm(z.reshape(B, S, CONV_GROUPS, gsz),
                  norm_g.reshape(CONV_GROUPS, gsz), norm_b.reshape(CONV_GROUPS, gsz))
    return jax.nn.silu(z).reshape(B, S, CONV_CH)


def chunked_sgu(h, v_norm_g, v_norm_b, w_s, b_s):
    B, S, _ = h.shape
    z = jax.nn.gelu(h)
    u, v = z[..., :GM_WIDTH], z[..., GM_WIDTH:]
    v = layernorm(v, v_norm_g, v_norm_b)
    v = v.reshape(B, S // CHUNK, CHUNK, GM_GROUPS, GM_WIDTH // GM_GROUPS)
    w = w_s * jnp.tril(jnp.ones((CHUNK, CHUNK), w_s.dtype))[None]
    s = jnp.einsum('gts,bcsgd->bctgd', w, v) + b_s.T[None, None, :, :, None]
    return u * s.reshape(B, S, GM_WIDTH)


def setup_inputs(seed: int = 0) -> dict:
    key = jax.random.key(seed)
    ks = iter(jax.random.split(key, 48))
    f32 = jnp.float32

    def w(shape, fan_in):
        return jax.random.normal(next(ks), shape, f32) * fan_in ** -0.5

    def gain(shape):
        return 1.0 + 0.02 * jax.random.normal(next(ks), shape, f32)

    def small(shape):
        return 0.02 * jax.random.normal(next(ks), shape, f32)

    x = jax.random.normal(next(ks), (BATCH, SEQ, D_MODEL), f32)
    offs = jax.random.randint(next(ks), (BATCH, 1), 0, 4096, dtype=jnp.int32)
    positions = (offs + jnp.arange(SEQ, dtype=jnp.int32)[None, :]).astype(jnp.int32)
    return {
        "x": x,
        "positions": positions,
        "ffn_a_pre_g": gain((DEPTH, D_MODEL)),
        "ffn_a_post_g": gain((DEPTH, D_MODEL)),
        "ffn_a_w_gate": w((DEPTH, D_MODEL, D_FF), D_MODEL),
        "ffn_a_w_up": w((DEPTH, D_MODEL, D_FF), D_MODEL),
        "ffn_a_w_down": w((DEPTH, D_FF, D_MODEL), D_FF),
        "ffn_b_pre_g": gain((DEPTH, D_MODEL)),
        "ffn_b_post_g": gain((DEPTH, D_MODEL)),
        "ffn_b_w_gate": w((DEPTH, D_MODEL, D_FF), D_MODEL),
        "ffn_b_w_up": w((DEPTH, D_MODEL, D_FF), D_MODEL),
        "ffn_b_w_down": w((DEPTH, D_FF, D_MODEL), D_FF),
        "even_pre_g": gain((N_EVEN, D_MODEL)),
        "even_post_g": gain((N_EVEN, D_MODEL)),
        "even_w_in": w((N_EVEN, D_MODEL, IN_EVEN), D_MODEL),
        "even_q_norm_g": gain((N_EVEN, Q_LORA)),
        "even_kv_norm_g": gain((N_EVEN, KV_LORA)),
        "even_w_uq": w((N_EVEN, Q_LORA, MLA_HEADS * (QK_NOPE + QK_ROPE)), Q_LORA),
        "even_w_ukv": w((N_EVEN, KV_LORA, MLA_HEADS * (QK_NOPE + V_HEAD)), KV_LORA),
        "even_conv_w": w((N_EVEN, CONV_WIDTH, CONV_CH), CONV_WIDTH),
        "even_conv_b": small((N_EVEN, CONV_CH)),
        "even_conv_norm_g": gain((N_EVEN, CONV_CH)),
        "even_conv_norm_b": small((N_EVEN, CONV_CH)),
        "even_w_out": w((N_EVEN, D_MODEL, D_MODEL), D_MODEL),
        "odd_pre_g": gain((N_ODD, D_MODEL)),
        "odd_post_g": gain((N_ODD, D_MODEL)),
        "odd_w_in": w((N_ODD, D_MODEL, 2 * GM_WIDTH), D_MODEL),
        "odd_v_norm_g": gain((N_ODD, GM_WIDTH)),
        "odd_v_norm_b": small((N_ODD, GM_WIDTH)),
        "odd_w_s": w((N_ODD, GM_GROUPS, CHUNK, CHUNK), CHUNK),
        "odd_b_s": gain((N_ODD, GM_GROUPS, CHUNK)),
        "odd_w_out": w((N_ODD, GM_WIDTH, D_MODEL), GM_WIDTH),
    }


def reference(x, positions,
              ffn_a_pre_g, ffn_a_post_g, ffn_a_w_gate, ffn_a_w_up, ffn_a_w_down,
              ffn_b_pre_g, ffn_b_post_g, ffn_b_w_gate, ffn_b_w_up, ffn_b_w_down,
              even_pre_g, even_post_g, even_w_in, even_q_norm_g, even_kv_norm_g,
              even_w_uq, even_w_ukv, even_conv_w, even_conv_b, even_conv_norm_g,
              even_conv_norm_b, even_w_out,
              odd_pre_g, odd_post_g, odd_w_in, odd_v_norm_g, odd_v_norm_b,
              odd_w_s, odd_b_s, odd_w_out):
    inv_freq = ROPE_THETA ** (-jnp.arange(0, QK_ROPE, 2, dtype=jnp.float32) / QK_ROPE)
    ang = positions.astype(jnp.float32)[..., None] * inv_freq
    cos, sin = jnp.cos(ang).astype(x.dtype), jnp.sin(ang).astype(x.dtype)

    for l in range(DEPTH):
        h = rmsnorm(x, ffn_a_pre_g[l])
        x = x + 0.5 * rmsnorm(swiglu(h, ffn_a_w_gate[l], ffn_a_w_up[l], ffn_a_w_down[l]), ffn_a_post_g[l])
        if l % 2 == 0:
            i = l // 2
            h = rmsnorm(x, even_pre_g[i])
            p = h @ even_w_in[i]
            a = mla_attention(p[..., :OFF_KV], p[..., OFF_KV:OFF_KR], p[..., OFF_KR:OFF_CONV],
                              cos, sin, even_q_norm_g[i], even_kv_norm_g[i],
                              even_w_uq[i], even_w_ukv[i])
            c = conformer_conv(p[..., OFF_CONV:], even_conv_w[i], even_conv_b[i],
                               even_conv_norm_g[i], even_conv_norm_b[i])
            y = jnp.concatenate([a, c], axis=-1) @ even_w_out[i]
            x = x + rmsnorm(y, even_post_g[i])
        else:
            i = l // 2
            h = rmsnorm(x, odd_pre_g[i])
            y = chunked_sgu(h @ odd_w_in[i], odd_v_norm_g[i], odd_v_norm_b[i],
                            odd_w_s[i], odd_b_s[i]) @ odd_w_out[i]
            x = x + rmsnorm(y, odd_post_g[i])
        h = rmsnorm(x, ffn_b_pre_g[l])
        x = x + 0.5 * rmsnorm(swiglu(h, ffn_b_w_gate[l], ffn_b_w_up[l], ffn_b_w_down[l]), ffn_b_post_g[l])
    return x
```

```python
import os
import numpy as np
import concourse.bass as bass
import concourse.mybir as mybir
from concourse.bass_utils import run_bass_kernel_spmd

F32 = mybir.dt.float32
BF16 = mybir.dt.bfloat16
I32 = mybir.dt.int32
ALU = mybir.AluOpType
AF = mybir.ActivationFunctionType
AX = mybir.AxisListType

D = 2048
DFF = 5632
SEQ = 2048
NTOK = 4096
T = 512
KC = D // 128
FC = DFF // 128
EPS = 1e-6
IN_EVEN = 3136
N_CORES = 8

ENGS = ("pe", "act", "dve", "pool", "sp")
SEM_ROLL = 20000
PIPE_POST = int(os.environ.get('K_PIPE_POST', '1'))
PIPE_PRE = int(os.environ.get('K_PIPE_PRE', '1'))
ACCUM_EVEN = int(os.environ.get('K_ACCUM_EVEN', '0'))


class Op:
    __slots__ = ("eng", "fn", "reads", "writes", "dma", "deps", "signal", "sem", "val")

    def __init__(self, eng, fn, reads, writes, dma):
        self.eng = eng
        self.fn = fn
        self.reads = reads
        self.writes = writes
        self.dma = dma
        self.deps = []
        self.signal = False
        self.sem = None
        self.val = 0


class Prog:
    def __init__(self, nc, n_dma_sems=16, same_engine_sync=True):
        self.nc = nc
        self.ops = []
        self.last_writer = {}
        self.readers = {}
        self.n_dma_sems = n_dma_sems
        self.same_engine_sync = same_engine_sync
        self.last_ops = {e: None for e in ENGS}
        self.recent_dma = {e: [] for e in ENGS}
        self.pending_bar = {e: [] for e in ENGS}

    def op(self, eng, fn, reads=(), writes=(), dma=False):
        o = Op(eng, fn, tuple(reads), tuple(writes), dma)
        deps = set()
        for k in o.reads:
            w = self.last_writer.get(k)
            if w is not None:
                deps.add(w)
        for k in o.writes:
            w = self.last_writer.get(k)
            if w is not None:
                deps.add(w)
            for r in self.readers.get(k, ()):
                deps.add(r)
        for k in o.writes:
            self.last_writer[k] = o
            self.readers[k] = []
        for k in o.reads:
            self.readers.setdefault(k, []).append(o)
        deps.discard(o)
        for d in deps:
            if d.eng == o.eng and not d.dma and not o.dma:
                if o.eng == "pe" or not self.same_engine_sync:
                    continue
            o.deps.append(d)
            d.signal = True
        if self.pending_bar[eng]:
            for d in self.pending_bar[eng]:
                if d.eng == eng and not d.dma:
                    continue
                o.deps.append(d)
                d.signal = True
            self.pending_bar[eng] = []
        self.ops.append(o)
        if dma:
            self.recent_dma[eng].append(o)
            if len(self.recent_dma[eng]) > self.n_dma_sems:
                self.recent_dma[eng].pop(0)
        else:
            self.last_ops[eng] = o
        return o

    def barrier(self):
        bar = []
        for e in ENGS:
            if self.last_ops[e] is not None:
                bar.append(self.last_ops[e])
            bar.extend(self.recent_dma[e])
        for e in ENGS:
            self.pending_bar[e] = list(bar)
        self.last_writer = {}
        self.readers = {}

    def emit(self):
        nc = self.nc
        per_eng = {e: [] for e in ENGS}
        for o in self.ops:
            per_eng[o.eng].append(o)

        dma_prev = {}
        for e in ENGS:
            cur = None
            cnt = 0
            nroll = 0
            pool_sems = None
            ndma = 0
            for o in per_eng[e]:
                if o.dma:
                    if pool_sems is None:
                        pool_sems = [nc.alloc_semaphore(name=f"d_{e}_{i}") for i in range(self.n_dma_sems)]
                    s = pool_sems[ndma % self.n_dma_sems]
                    rnd = ndma // self.n_dma_sems
                    o.sem = s
                    o.val = 16 * (rnd + 1)
                    if rnd > 0:
                        dma_prev[o] = (s, 16 * rnd)
                    ndma += 1
                elif o.signal:
                    if cur is None or cnt >= SEM_ROLL:
                        cur = nc.alloc_semaphore(name=f"c_{e}_{nroll}")
                        nroll += 1
                        cnt = 0
                    cnt += 1
                    o.sem = cur
                    o.val = cnt
        final_waits = []
        for e in ENGS:
            last = {}
            for o in per_eng[e]:
                if o.dma:
                    last[id(o.sem)] = o
            final_waits.extend(last.values())
        stats = {"waits": 0, "incs": 0, "ops": len(self.ops)}

        def run_engine(e, eng):
            known = {}
            for o in per_eng[e]:
                need = {}
                if o in dma_prev:
                    s, v = dma_prev[o]
                    need[id(s)] = (s, v)
                for d in o.deps:
                    k = id(d.sem)
                    if k not in need or need[k][1] < d.val:
                        need[k] = (d.sem, d.val)
                for k, (s, v) in need.items():
                    if known.get(k, 0) >= v:
                        continue
                    eng.wait_ge(s, v)
                    known[k] = v
                    stats["waits"] += 1
                ins = o.fn(eng)
                if o.dma:
                    ins.then_inc(o.sem, 16)
                elif o.signal:
                    ins.then_inc(o.sem, 1)
                    stats["incs"] += 1
            if e == "sp":
                for o in final_waits:
                    if known.get(id(o.sem), 0) >= o.val:
                        continue
                    eng.wait_ge(o.sem, o.val)

        with nc.Block() as block:
            @block.tensor
            def _(eng):
                run_engine("pe", eng)

            @block.scalar
            def _(eng):
                run_engine("act", eng)

            @block.vector
            def _(eng):
                run_engine("dve", eng)

            @block.gpsimd
            def _(eng):
                run_engine("pool", eng)

            @block.sync
            def _(eng):
                run_engine("sp", eng)
        return stats


COLS = {}
_off = 0
for _n, _w in [("ffn_a_pre_g0", 16), ("ffn_a_post_g0", 16), ("ffn_b_pre_g0", 16), ("ffn_b_post_g0", 16),
               ("ffn_a_pre_g1", 16), ("ffn_a_post_g1", 16), ("ffn_b_pre_g1", 16), ("ffn_b_post_g1", 16),
               ("even_pre_g", 16), ("even_post_g", 16), ("odd_pre_g", 16), ("odd_post_g", 16),
               ("q_norm_g", 4), ("kv_norm_g", 4), ("conv_w", 248), ("conv_b", 8), ("conv_ng", 8),
               ("conv_nb", 8), ("inv_freq", 1), ("sin_sign", 1)]:
    COLS[_n] = (_off, _w)
    _off += _w
NCOL = _off


class Ctx:
    pass


def build_program(stages, tiles, debug_out=None):
    nc = bass.Bass("TRN2", target_bir_lowering=False)
    P = Prog(nc)
    C = Ctx()
    C.nc, C.P = nc, P

    def din(name, shape, dt=F32):
        return nc.dram_tensor(name, list(shape), dt, kind="ExternalInput").ap()

    C.x_in = din("x", [NTOK, D])
    C.pos = din("pos", [1, NTOK], I32)
    C.cols_d = din("cols", [128, NCOL])
    C.w = {}
    for l in range(2):
        for ab in "ab":
            C.w[f"ffn_{ab}_w_gate{l}"] = din(f"ffn_{ab}_w_gate{l}", [D, DFF])
            C.w[f"ffn_{ab}_w_up{l}"] = din(f"ffn_{ab}_w_up{l}", [D, DFF])
            C.w[f"ffn_{ab}_w_down{l}"] = din(f"ffn_{ab}_w_down{l}", [DFF, D])
    C.w["even_w_in"] = din("even_w_in", [D, IN_EVEN])
    C.w["even_w_uq"] = din("even_w_uq", [512, 1536])
    C.w["even_w_ukv"] = din("even_w_ukv", [512, 2048])
    C.w["even_w_out"] = din("even_w_out", [D, D])
    C.w["odd_w_in"] = din("odd_w_in", [D, 2 * D])
    C.w["odd_w_out"] = din("odd_w_out", [D, D])
    C.odd_vg = din("odd_v_norm_g", [1, D])
    C.odd_vb = din("odd_v_norm_b", [1, D])
    C.odd_ws = din("odd_w_s", [8, 128, 128])
    C.odd_bs = din("odd_b_s", [1, 8 * 128])
    C.y_out = nc.dram_tensor("y", [NTOK, D], F32, kind="ExternalOutput").ap()
    C.xT = nc.dram_tensor("xT_scratch", [D, NTOK], F32).ap()
    C.xTv = C.xT.rearrange("(k p) t -> p k t", p=128)

    cnt = [0]

    def sb(name, shape, dt=F32):
        cnt[0] += 1
        return nc.alloc_sbuf_tensor(f"{name}_{cnt[0]}", list(shape), dt)

    C.sb = sb
    C.cols = sb("cols_sb", [128, NCOL])
    C.ones_f = sb("ones_f", [128, 128])
    C.ones_b = sb("ones_b", [128, 128], BF16)
    C.ident_f = sb("ident_f", [128, 128])
    C.ident_b = sb("ident_b", [128, 128], BF16)
    C.ps = nc.alloc_psum_tensor("ps", [128, 8, 512], F32)

    P.op("sp", lambda e: e.dma_start(out=C.cols[:], in_=C.cols_d), writes=["cols"], dma=True)
    P.op("dve", lambda e: e.memset(C.ones_f[:], 1.0), writes=["ones_f"])
    P.op("dve", lambda e: e.memset(C.ones_b[:], 1.0), writes=["ones_b"])
    P.op("pool", lambda e: e.memset(C.ident_f[:], 1.0), writes=["ident_f"])
    P.op("pool", lambda e: e.affine_select(out=C.ident_f[:], in_=C.ident_f[:], pattern=[[-1, 128]],
                                           compare_op=ALU.is_equal, fill=0.0, base=0, channel_multiplier=1),
         reads=["ident_f"], writes=["ident_f"])
    P.op("dve", lambda e: e.tensor_copy(C.ident_b[:], C.ident_f[:]), reads=["ident_f"], writes=["ident_b"])

    C.fcache = {}
    C.fconv_done = set()
    C.fconv_claimed = set()
    for l in range(2):
        for ab in "ab":
            C.fcache[(ab, l)] = {
                "g": nc.dram_tensor(f"wc_g_{ab}{l}", [DFF // 256, 128, KC * 256], BF16).ap(),
                "u": nc.dram_tensor(f"wc_u_{ab}{l}", [DFF // 256, 128, KC * 256], BF16).ap(),
                "d": nc.dram_tensor(f"wc_d_{ab}{l}", [KC, 128, FC * 128], BF16).ap(),
            }
    C.stages = list(stages)
    base0 = nc.sbuf_base
    for st in stages:
        nc.sbuf_base = base0
        P.barrier()
        if st == "tin":
            stage_transpose_in(C, tiles)
        elif st == "tout":
            stage_transpose_out(C, tiles)
        elif st.startswith("ffn"):
            ab, l = st[4], int(st[5])
            stage_ffn(C, tiles, ab, l, final=(st == stages[-1]))
        elif st == "odd":
            stage_odd(C, tiles)
        elif st == "even":
            stage_even(C, tiles)
        else:
            raise ValueError(st)
        print("stage", st, "sbuf bytes left", nc.sbuf_top - nc.sbuf_base)
    C.stats = P.emit()
    return nc, C


def col(C, name, i=0):
    off, w = COLS[name]
    return C.cols[:, off + i:off + i + 1]


def stage_transpose_in(C, tiles):
    nc, P = C.nc, C.P
    xin = [C.sb(f"ti_x{i}", [128, D]) for i in range(2)]
    xo = [C.sb(f"ti_o{i}", [128, KC, 128]) for i in range(2)]
    n = 0
    for t in tiles:
        for b in range(T // 128):
            t0 = t * T + b * 128
            s = n % 2
            P.op("sp", lambda e, s=s, t0=t0: e.dma_start(out=xin[s][:], in_=C.x_in[t0:t0 + 128, :]),
                 writes=[("ti_x", s)], dma=True)
            for g in range(4):
                bank = (n * 4 + g) % 8
                for kk in range(4):
                    k = g * 4 + kk
                    P.op("pe", lambda e, s=s, k=k, bank=bank, kk=kk: e.transpose(
                        C.ps[:, bank, kk * 128:(kk + 1) * 128], xin[s][:, k * 128:(k + 1) * 128], C.ident_f[:]),
                        reads=[("ti_x", s), "ident_f"], writes=[("ps", bank)])
                eng = "act" if g % 2 == 0 else "dve"
                if eng == "act":
                    P.op("act", lambda e, s=s, g=g, bank=bank: e.copy(
                        xo[s][:, g * 4:(g + 1) * 4, :], C.ps[:, bank, :].rearrange("p (a b) -> p a b", a=4)),
                        reads=[("ps", bank)], writes=[("ti_o", s, g)])
                else:
                    P.op("dve", lambda e, s=s, g=g, bank=bank: e.tensor_copy(
                        xo[s][:, g * 4:(g + 1) * 4, :], C.ps[:, bank, :].rearrange("p (a b) -> p a b", a=4)),
                        reads=[("ps", bank)], writes=[("ti_o", s, g)])
            P.op("sp", lambda e, s=s, t0=t0: e.dma_start(out=C.xTv[:, :, t0:t0 + 128], in_=xo[s][:]),
                 reads=[("ti_o", s, g) for g in range(4)], writes=[("xT", t0 // T, c) for c in range(KC)], dma=True)
            n += 1


def stage_transpose_out(C, tiles):
    nc, P = C.nc, C.P
    xin = [C.sb(f"to_x{i}", [128, KC, 128]) for i in range(2)]
    xo = [C.sb(f"to_o{i}", [128, D]) for i in range(2)]
    n = 0
    for t in tiles:
        for b in range(T // 128):
            t0 = t * T + b * 128
            s = n % 2
            P.op("sp", lambda e, s=s, t0=t0: e.dma_start(out=xin[s][:], in_=C.xTv[:, :, t0:t0 + 128]),
                 reads=[("xT", t0 // T, c) for c in range(KC)], writes=[("to_x", s)], dma=True)
            for g in range(4):
                bank = (n * 4 + g) % 8
                for kk in range(4):
                    k = g * 4 + kk
                    P.op("pe", lambda e, s=s, k=k, bank=bank, kk=kk: e.transpose(
                        C.ps[:, bank, kk * 128:(kk + 1) * 128], xin[s][:, k, :], C.ident_f[:]),
                        reads=[("to_x", s), "ident_f"], writes=[("ps", bank)])
                if g % 2 == 0:
                    P.op("act", lambda e, s=s, g=g, bank=bank: e.copy(
                        xo[s][:, g * 512:(g + 1) * 512], C.ps[:, bank, :]),
                        reads=[("ps", bank)], writes=[("to_o", s, g)])
                else:
                    P.op("dve", lambda e, s=s, g=g, bank=bank: e.tensor_copy(
                        xo[s][:, g * 512:(g + 1) * 512], C.ps[:, bank, :]),
                        reads=[("ps", bank)], writes=[("to_o", s, g)])
            P.op("sp", lambda e, s=s, t0=t0: e.dma_start(out=C.y_out[t0:t0 + 128, :], in_=xo[s][:]),
                 reads=[("to_o", s, g) for g in range(4)], writes=[("y", t0)], dma=True)
            n += 1


class Rot:
    def __init__(self, name, bufs):
        self.name, self.bufs, self.i = name, bufs, 0

    def next(self):
        s = self.i % len(self.bufs)
        self.i += 1
        return self.bufs[s], (self.name, s)


def rstd_from_sum(C, ps_ap, width, n_feat, scratch, key_in, key_out, parts=128):
    P = C.P
    P.op("dve", lambda e: e.tensor_scalar(scratch[0:parts, 0:width], ps_ap, 1.0 / n_feat, EPS, ALU.mult, ALU.add),
         reads=[key_in], writes=[key_out])
    P.op("act", lambda e: e.activation(out=scratch[0:parts, 0:width], in_=scratch[0:parts, 0:width], func=AF.Sqrt),
         reads=[key_out], writes=[key_out])
    P.op("dve", lambda e: e.reciprocal(scratch[0:parts, 0:width], scratch[0:parts, 0:width]),
         reads=[key_out], writes=[key_out])


def prenorm_gen(C, R, t, gname, hT, hkey):
    P = C.P
    for b in range(T // 128):
        t0 = t * T + b * 128
        xt, xk = R.xt.next()
        P.op("sp", lambda e, xt=xt, t0=t0: e.dma_start(out=xt[:], in_=C.xTv[:, :, t0:t0 + 128]),
             reads=[("xT", t, c) for c in range(KC)], writes=[xk], dma=True)
        bank = 6
        split = len(R.sq.bufs) >= KC // 4

        def mm(sq, sk, q):
            for kk in range(4):
                k = q * 4 + kk
                P.op("pe", lambda e, sq=sq, k=k, kk=kk, bank=bank: e.matmul(C.ps[:, bank, 0:128], lhsT=C.ones_b[:], rhs=sq[:, kk, :],
                                                                             start=(k == 0), stop=(k == KC - 1)),
                     reads=[sk, "ones_b"], writes=[("ps", bank)])

        sqs = []
        for q in range(KC // 4):
            sq, sk = R.sq.next()
            sqs.append((sq, sk))
            P.op("dve", lambda e, xt=xt, sq=sq, q=q: e.tensor_tensor(sq[:], xt[:, q * 4:(q + 1) * 4, :], xt[:, q * 4:(q + 1) * 4, :], ALU.mult),
                 reads=[xk], writes=[sk])
            if not split:
                mm(sq, sk, q)
        yield
        if split:
            for q in range(KC // 4):
                mm(sqs[q][0], sqs[q][1], q)
        rs, rk = R.rs.next()
        rstd_from_sum(C, C.ps[:, bank, 0:128], 128, D, rs, ("ps", bank), rk)
        for k in range(KC):
            P.op("dve", lambda e, xt=xt, rs=rs, k=k, b=b: e.scalar_tensor_tensor(
                out=hT[:, k, b * 128:(b + 1) * 128], in0=xt[:, k, :], scalar=col(C, gname, k), in1=rs[:, 0:128],
                op0=ALU.mult, op1=ALU.mult),
                reads=[xk, rk, "cols"], writes=[(hkey, k, b)])
        yield


def prenorm_tile(C, R, t, gname, hT, hkey):
    for _ in prenorm_gen(C, R, t, gname, hT, hkey):
        pass


def convert_gen(C, ab, l):
    P = C.P
    ch = C.fcache[(ab, l)]
    for key, wname, kc, gw, nsl in (("g", "gate", KC, 256, DFF // 256), ("u", "up", KC, 256, DFF // 256), ("d", "down", FC, 128, KC)):
        wv = C.w[f"ffn_{ab}_w_{wname}{l}"].rearrange("(k p) f -> p k f", p=128)
        for sl in range(nsl):
            dst = ch[key][sl].rearrange("p (k f) -> p k f", k=kc)
            P.op("pool", lambda e, dst=dst, wv=wv, sl=sl, gw=gw: e.dma_start(out=dst, in_=wv[:, :, sl * gw:(sl + 1) * gw]),
                 writes=[("wc", f"w{key}{ab}{l}", sl)], dma=True)
            yield
    C.fconv_done.add((ab, l))


def chain_gens(*gens):
    for g in gens:
        for _ in g:
            yield


def upcoming_ffns(C, st, n):
    i = C.stages.index(st)
    out = []
    for s2 in C.stages[i + 1:]:
        if s2.startswith("ffn") and (s2[4], int(s2[5])) not in C.fconv_claimed:
            out.append((s2[4], int(s2[5])))
            if len(out) == n:
                break
    for k in out:
        C.fconv_claimed.add(k)
    return out


class WStream:
    def __init__(self, C, name, kc, gw, nbuf, n_slabs=0, cache=None):
        self.C = C
        self.name = name
        self.kc, self.gw = kc, gw
        self.rot = Rot(name, [C.sb(f"{name}{i}", [128, kc, gw], BF16) for i in range(nbuf)])
        self.cache = None
        self.pending = None
        if cache is not None:
            self.cache = cache
        elif n_slabs:
            C.ncache = getattr(C, "ncache", 0) + 1
            self.cache = C.nc.dram_tensor(f"wc_{name}_{C.ncache}", [n_slabs, 128, kc * gw], BF16).ap()

    def load(self, w_ap, c0, width=None, slab=None, first=True):
        C = self.C
        P = C.P
        width = width or self.gw
        buf, key = self.rot.next()
        if self.cache is not None and not first:
            src = self.cache[slab].rearrange("p (k f) -> p k f", k=self.kc)
            P.op("pool", lambda e: e.dma_start(out=buf[:], in_=src), reads=[("wc", self.name, slab)], writes=[key], dma=True)
            return buf, key
        wv = w_ap.rearrange("(k p) f -> p k f", p=128)
        P.op("pool", lambda e: e.dma_start(out=buf[:, :, 0:width], in_=wv[:, :, c0:c0 + width]),
             writes=[key], dma=True)
        self.flush()
        if self.cache is not None:
            dst = self.cache[slab].rearrange("p (k f) -> p k f", k=self.kc)
            self.pending = (buf, key, dst, slab)
        return buf, key

    def flush(self):
        if self.pending is not None:
            buf, key, dst, slab = self.pending
            self.pending = None
            self.C.P.op("pool", lambda e: e.dma_start(out=dst, in_=buf[:]), reads=[key], writes=[("wc", self.name, slab)], dma=True)


def postnorm_gen(C, R, t, YT, ykeyf, gname, gscale, nchunks=KC, order=None, final=False, accum=False):
    P = C.P
    bank = 7
    for c in range(nchunks):
        sq, sk = R.sq2.next()
        P.op("dve", lambda e, sq=sq, c=c: e.tensor_tensor(sq[:], YT[:, c, :], YT[:, c, :], ALU.mult),
             reads=[ykeyf(c)], writes=[sk])
        P.op("pe", lambda e, sq=sq, c=c: e.matmul(C.ps[:, bank, :], lhsT=C.ones_b[:], rhs=sq[:],
                                                   start=(c == 0), stop=(c == nchunks - 1)),
             reads=[sk, "ones_b"], writes=[("ps", bank)])
        if c % 4 == 3:
            yield
    rs, rk = R.rs2.next()
    rstd_from_sum(C, C.ps[:, bank, :], T, D, rs, ("ps", bank), rk)
    if gscale != 1.0:
        P.op("dve", lambda e, rs=rs: e.tensor_scalar(rs[:, 0:T], rs[:, 0:T], float(gscale), None, ALU.mult),
             reads=[rk], writes=[rk])
    yield
    yv = C.y_out.rearrange("(n p) d -> p n d", p=128)

    def emit_out(xr, xk, c):
        for bq in range(T // 128):
            P.op("pe", lambda e, xr=xr, bq=bq: e.transpose(C.ps[:, bank, bq * 128:(bq + 1) * 128], xr[:, bq * 128:(bq + 1) * 128],
                                                            C.ident_f[:]),
                 reads=[xk, "ident_f"], writes=[("ps", bank)])
        ob, obk = R.ob.next()
        P.op("act", lambda e, ob=ob: e.copy(ob[:], C.ps[:, bank, :].rearrange("p (a b) -> p a b", a=T // 128)),
             reads=[("ps", bank)], writes=[obk])
        P.op("sp", lambda e, ob=ob, c=c: e.dma_start(out=yv[:, t * (T // 128):(t + 1) * (T // 128), c * 128:(c + 1) * 128], in_=ob[:]),
             reads=[obk], writes=[("y", t, c)], dma=True)

    prev = None
    for c in (order if order is not None else range(nchunks)):
        if accum:
            P.op("dve", lambda e, c=c, rs=rs: e.scalar_tensor_tensor(
                out=YT[:, c, :], in0=YT[:, c, :], scalar=col(C, gname, c), in1=rs[:, 0:T], op0=ALU.mult, op1=ALU.mult),
                reads=[ykeyf(c), rk, "cols"], writes=[ykeyf(c)])
            P.op("pool", lambda e, c=c: e.dma_start(out=C.xTv[:, c, t * T:(t + 1) * T], in_=YT[:, c, :], accum_op=ALU.add),
                 reads=[ykeyf(c)], writes=[("xT", t, c)], dma=True)
            yield
            continue
        xr, xk = R.xr.next()
        P.op("sp", lambda e, xr=xr, c=c: e.dma_start(out=xr[:], in_=C.xTv[:, c, t * T:(t + 1) * T]),
             reads=[("xT", t, c)], writes=[xk], dma=True)
        P.op("dve", lambda e, c=c, rs=rs: e.scalar_tensor_tensor(
            out=YT[:, c, :], in0=YT[:, c, :], scalar=col(C, gname, c), in1=rs[:, 0:T], op0=ALU.mult, op1=ALU.mult),
            reads=[ykeyf(c), rk, "cols"], writes=[ykeyf(c)])
        P.op("dve", lambda e, c=c, xr=xr: e.tensor_tensor(xr[:], xr[:], YT[:, c, :], ALU.add),
             reads=[ykeyf(c), xk], writes=[xk])
        if not final:
            P.op("sp", lambda e, xr=xr, c=c: e.dma_start(out=C.xTv[:, c, t * T:(t + 1) * T], in_=xr[:]),
                 reads=[xk], writes=[("xT", t, c)], dma=True)
        else:
            if prev is not None:
                emit_out(*prev)
            prev = (xr, xk, c)
        yield
    if prev is not None:
        emit_out(*prev)
        yield


def postnorm_residual_tile(C, R, t, YT, ykeyf, gname, gscale, nchunks=KC, accum=False):
    for _ in postnorm_gen(C, R, t, YT, ykeyf, gname, gscale, nchunks, accum=accum):
        pass


def norm_bufs(C, R, nxt=2, nsq=2, nxr=2):
    R.xt = Rot("xt", [C.sb(f"xt{i}", [128, KC, 128]) for i in range(nxt)])
    R.sq = Rot("sq", [C.sb(f"sq{i}", [128, 4, 128], BF16) for i in range(nsq)])
    R.rs = Rot("rs", [C.sb(f"rs{i}", [128, 128]) for i in range(2)])
    R.sq2 = Rot("sq2", [C.sb(f"sq2{i}", [128, T], BF16) for i in range(3)])
    R.rs2 = Rot("rs2", [C.sb(f"rs2{i}", [128, T]) for i in range(1)])
    R.xr = Rot("xr", [C.sb(f"xr{i}", [128, T]) for i in range(nxr)])


def stage_ffn(C, tiles, ab, l, final=False):
    nc, P = C.nc, C.P
    R = Ctx()
    st_name = f"ffn_{ab}{l}"
    idx = C.stages.index(st_name)
    todo = upcoming_ffns(C, st_name, 1) if (idx + 1 < len(C.stages) and C.stages[idx + 1] == "odd") else []
    bgc = chain_gens(*[convert_gen(C, ab_, l_) for ab_, l_ in todo])
    bg_every = max(1, (22 * max(1, len(tiles))) // (60 * len(todo))) if todo else 0
    norm_bufs(C, R, nsq=4)
    if final:
        R.ob = Rot("ob", [C.sb(f"ob{i}", [128, T // 128, 128]) for i in range(2)])
        R.xr = Rot("xr", R.xr.bufs + [C.sb("xr_extra", [128, T])])
    R.pending_post = None
    hTs = [C.sb(f"hT{i}", [128, KC, T], BF16) for i in range(2)]
    AT = C.sb("AT", [128, FC, T], BF16)
    YT = C.sb("YT", [128, KC, T])
    GW = 256
    ch = C.fcache[(ab, l)]
    pre_done = (ab, l) in C.fconv_done
    wg_s = WStream(C, f"wg{ab}{l}", KC, GW, 2, cache=ch["g"])
    wu_s = WStream(C, f"wu{ab}{l}", KC, GW, 2, cache=ch["u"])
    wd_s = WStream(C, f"wd{ab}{l}", FC, 128, 2, cache=ch["d"])
    silu_t = Rot("silu", [C.sb(f"silu{i}", [128, T]) for i in range(2)])
    wg, wu, wd = (C.w[f"ffn_{ab}_w_{n}{l}"] for n in ("gate", "up", "down"))
    pre, post = f"ffn_{ab}_pre_g{l}", f"ffn_{ab}_post_g{l}"

    def tile_body(ti, t):
        hT = hTs[ti % 2]
        hkey = ("hT", ti % 2)
        if ti == 0:
            prenorm_tile(C, R, t, pre, hT, hkey)
        pend = R.pending_post
        R.pending_post = None
        if pend is not None and not PIPE_POST:
            for _ in pend:
                pass
            pend = None
        hreads = [(hkey, k, b) for k in range(KC) for b in range(T // 128)]
        n = 0
        for g in range(DFF // GW):
            gb, gk = wg_s.load(wg, g * GW, slab=g, first=(ti == 0 and not pre_done))
            ub, uk = wu_s.load(wu, g * GW, slab=g, first=(ti == 0 and not pre_done))
            for jj in range(GW // 128):
                j = g * (GW // 128) + jj
                bg, bu = (n % 2) * 2, (n % 2) * 2 + 1
                n += 1
                for k in range(KC):
                    P.op("pe", lambda e, gb=gb, k=k, jj=jj, bg=bg: e.matmul(
                        C.ps[:, bg, :], lhsT=gb[:, k, jj * 128:(jj + 1) * 128], rhs=hT[:, k, :],
                        start=(k == 0), stop=(k == KC - 1)),
                        reads=[gk] + (hreads if k in (0, KC - 1) else []), writes=[("ps", bg)])
                for k in range(KC):
                    P.op("pe", lambda e, ub=ub, k=k, jj=jj, bu=bu: e.matmul(
                        C.ps[:, bu, :], lhsT=ub[:, k, jj * 128:(jj + 1) * 128], rhs=hT[:, k, :],
                        start=(k == 0), stop=(k == KC - 1)),
                        reads=[uk] + (hreads if k == KC - 1 else []), writes=[("ps", bu)])
                st, sk = silu_t.next()
                P.op("act", lambda e, st=st, bg=bg: e.activation(out=st[:], in_=C.ps[:, bg, :], func=AF.Silu),
                     reads=[("ps", bg)], writes=[sk])
                P.op("dve", lambda e, st=st, bu=bu, j=j: e.tensor_tensor(AT[:, j, :], st[:], C.ps[:, bu, :], ALU.mult),
                     reads=[sk, ("ps", bu)], writes=[("AT", j)])
            if pend is not None:
                next(pend, None)
            if bg_every and g % bg_every == 0:
                next(bgc, None)
        if pend is not None:
            for _ in pend:
                pass
        pre_gen = None
        if ti + 1 < len(tiles):
            pre_gen = prenorm_gen(C, R, tiles[ti + 1], pre, hTs[(ti + 1) % 2], ("hT", (ti + 1) % 2))
            if PIPE_PRE:
                next(pre_gen, None)
            else:
                for _ in pre_gen:
                    pass
                pre_gen = None
        areads = [("AT", j) for j in range(FC)]
        for c in range(KC):
            db, dk = wd_s.load(wd, c * 128, slab=c, first=(ti == 0 and not pre_done))
            bank = 4 + c % 2
            for j in range(FC):
                P.op("pe", lambda e, db=db, j=j, bank=bank: e.matmul(
                    C.ps[:, bank, :], lhsT=db[:, j, :], rhs=AT[:, j, :], start=(j == 0), stop=(j == FC - 1)),
                    reads=[dk] + (areads if j in (0, FC - 1) else []), writes=[("ps", bank)])
            P.op("act", lambda e, c=c, bank=bank: e.copy(YT[:, c, :], C.ps[:, bank, :]),
                 reads=[("ps", bank)], writes=[("YT", c)])
            if pre_gen is not None and c % 3 == 1:
                next(pre_gen, None)
                next(pre_gen, None)
        if pre_gen is not None:
            for _ in pre_gen:
                pass
        for ws__ in (wg_s, wu_s, wd_s):
            ws__.flush()
        R.pending_post = postnorm_gen(C, R, t, YT, lambda c: ("YT", c), post, 0.5, final=final)
        if ti == len(tiles) - 1:
            for _ in R.pending_post:
                pass
            R.pending_post = None

    for ti_, t_ in enumerate(tiles):
        tile_body(ti_, t_)
    for _ in bgc:
        pass


def stage_odd(C, tiles):
    nc, P = C.nc, C.P
    R = Ctx()
    norm_bufs(C, R, nsq=4, nxr=6)
    R.pending_post = None
    NB = T // 128
    hTs = [C.sb(f"o_hT{i}", [128, KC, T], BF16) for i in range(2)]
    big = C.sb("o_big", [128, KC, T])
    YT = big

    def vgv(b):
        return big[:, b * 4:(b + 1) * 4, :]

    def bk(slot):
        return ("o_big", slot)

    vln = C.sb("o_vln", [128, NB, D], BF16)
    gT = C.sb("o_gT", [128, KC, T], BF16)
    vgb = C.sb("o_vgb", [128, D])
    vbb = C.sb("o_vbb", [128, D])
    bsb = C.sb("o_bsb", [128, 8 * 128])
    wsT = C.sb("o_wsT", [128, 8, 128], BF16)
    wsl = C.sb("o_wsl", [128, 8, 128])
    stat = C.sb("o_stat", [128, 8])
    tmps = Rot("o_tmps", [C.sb(f"o_tmps{i}", [128, T]) for i in range(2)])
    ugel = Rot("o_ugel", [C.sb(f"o_ugel{i}", [128, T]) for i in range(2)])
    ws_ = WStream(C, "o_w", KC, 256, 4, n_slabs=24)
    w_in, w_out = C.w["odd_w_in"], C.w["odd_w_out"]
    todo = []
    bg = chain_gens(*[convert_gen(C, ab_, l_) for ab_, l_ in todo])
    bg_steps = -(-60 * len(todo) // (8 * max(1, len(tiles))))

    P.op("sp", lambda e: e.dma_start(out=vgb[:], in_=C.odd_vg.partition_broadcast(128)), writes=["vgb"], dma=True)
    P.op("sp", lambda e: e.dma_start(out=vbb[:], in_=C.odd_vb.partition_broadcast(128)), writes=["vbb"], dma=True)
    P.op("sp", lambda e: e.dma_start(out=bsb[:], in_=C.odd_bs.partition_broadcast(128)), writes=["bsb"], dma=True)
    P.op("sp", lambda e: e.dma_start(out=wsl[:], in_=C.odd_ws.rearrange("g t s -> t g s")), writes=["wsl"], dma=True)
    for g in range(8):
        bank = g % 2
        P.op("pe", lambda e, g=g, bank=bank: e.transpose(C.ps[:, bank, 0:128], wsl[:, g, :], C.ident_f[:]),
             reads=["wsl", "ident_f"], writes=[("ps", bank)])
        P.op("dve", lambda e, g=g, bank=bank: e.tensor_copy(wsl[:, g, :], C.ps[:, bank, 0:128]),
             reads=[("ps", bank)], writes=["wsl"])
        P.op("pool", lambda e, g=g: e.affine_select(out=wsl[:, g, :], in_=wsl[:, g, :], pattern=[[1, 128]],
                                                    compare_op=ALU.is_ge, fill=0.0, base=0, channel_multiplier=-1),
             reads=["wsl"], writes=["wsl"])
        P.op("dve", lambda e, g=g: e.tensor_copy(wsT[:, g, :], wsl[:, g, :]), reads=["wsl"], writes=[("wsT", g)])

    def tile_body(ti, t):
        hT = hTs[ti % 2]
        hkey = ("o_hT", ti % 2)
        if ti == 0:
            prenorm_tile(C, R, t, "odd_pre_g", hT, hkey)
        hreads = [(hkey, k, b) for k in range(KC) for b in range(NB)]
        pend = R.pending_post
        R.pending_post = None
        if pend is not None:
            for _ in range(4):
                next(pend, None)
        n = 0
        for nb_ in range(8):
            vb_, vk = ws_.load(w_in, D + nb_ * 256, slab=nb_, first=(ti == 0))
            for b in range(NB):
                bank = 2 + n % 2
                n += 1
                if pend is not None and nb_ % 2 == 0:
                    next(pend, None)
                for k in range(KC):
                    P.op("pe", lambda e, vb_=vb_, k=k, b=b, bank=bank: e.matmul(
                        C.ps[:, bank, 0:256], lhsT=hT[:, k, b * 128:(b + 1) * 128], rhs=vb_[:, k, :],
                        start=(k == 0), stop=(k == KC - 1)),
                        reads=[vk] + (hreads if k in (0, KC - 1) else []), writes=[("ps", bank)])
                slot = b * 4 + nb_ // 2
                P.op("act", lambda e, b=b, nb_=nb_, bank=bank, slot=slot: e.activation(
                    out=big[:, slot, (nb_ % 2) * 256:(nb_ % 2 + 1) * 256], in_=C.ps[:, bank, 0:256], func=AF.Gelu),
                    reads=[("ps", bank)], writes=[bk(slot)])
        if pend is not None:
            for _ in pend:
                pass
        for b in range(NB):
            vr = [bk(b * 4 + q) for q in range(4)]
            P.op("dve", lambda e, b=b: e.tensor_reduce(out=stat[:, 0:1], in_=vgv(b), axis=AX.XY, op=ALU.add),
                 reads=vr, writes=["o_stat0"])
            P.op("dve", lambda e: e.tensor_scalar(stat[:, 1:2], stat[:, 0:1], -1.0 / D, None, ALU.mult),
                 reads=["o_stat0"], writes=["o_stat1"])
            P.op("dve", lambda e, b=b: e.tensor_scalar(vgv(b), vgv(b), stat[:, 1:2], None, ALU.add),
                 reads=vr + ["o_stat1"], writes=vr)
            P.op("act", lambda e, b=b: e.activation(out=vln[:, b, :].rearrange("p (a c) -> p a c", a=4), in_=vgv(b),
                                                    func=AF.Square, accum_out=stat[:, 2:3]),
                 reads=vr, writes=[("o_vln", b), "o_stat2"])
            P.op("dve", lambda e: e.tensor_scalar(stat[:, 3:4], stat[:, 2:3], 1.0 / D, EPS, ALU.mult, ALU.add),
                 reads=["o_stat2"], writes=["o_stat3"])
            P.op("act", lambda e: e.activation(out=stat[:, 3:4], in_=stat[:, 3:4], func=AF.Sqrt),
                 reads=["o_stat3"], writes=["o_stat3"])
            P.op("dve", lambda e: e.reciprocal(stat[:, 4:5], stat[:, 3:4]), reads=["o_stat3"], writes=["o_stat4"])
            P.op("dve", lambda e, b=b: e.scalar_tensor_tensor(out=vgv(b), in0=vgv(b), scalar=stat[:, 4:5],
                                                              in1=vgb[:].rearrange("p (a c) -> p a c", a=4),
                                                              op0=ALU.mult, op1=ALU.mult),
                 reads=vr + ["o_stat4", "vgb"], writes=vr)
            P.op("dve", lambda e, b=b: e.tensor_tensor(vln[:, b, :].rearrange("p (a c) -> p a c", a=4), vgv(b),
                                                       vbb[:].rearrange("p (a c) -> p a c", a=4), ALU.add),
                 reads=vr + ["vbb"], writes=[("o_vln", b)])
        n = 0
        for g2 in range(D // 256):
            ub, uk = ws_.load(w_in, g2 * 256, slab=8 + g2, first=(ti == 0))
            for jj in range(2):
                j = g2 * 2 + jj
                g = j // 2
                bs_ = 4 + n % 2
                bu_ = n % 2
                n += 1
                for b in range(NB):
                    P.op("pe", lambda e, j=j, b=b, g=g, bs_=bs_: e.matmul(
                        C.ps[:, bs_, b * 128:(b + 1) * 128], lhsT=vln[:, b, j * 128:(j + 1) * 128], rhs=wsT[:, g, :],
                        start=True, stop=True),
                        reads=[("o_vln", b), ("wsT", g)], writes=[("ps", bs_)])
                for k in range(KC):
                    P.op("pe", lambda e, ub=ub, k=k, jj=jj, bu_=bu_: e.matmul(
                        C.ps[:, bu_, :], lhsT=ub[:, k, jj * 128:(jj + 1) * 128], rhs=hT[:, k, :],
                        start=(k == 0), stop=(k == KC - 1)),
                        reads=[uk] + (hreads if k in (0, KC - 1) else []), writes=[("ps", bu_)])
                ug, ugk = ugel.next()
                P.op("act", lambda e, ug=ug, bu_=bu_: e.activation(out=ug[:], in_=C.ps[:, bu_, :], func=AF.Gelu),
                     reads=[("ps", bu_)], writes=[ugk])
                ts_, tk = tmps.next()
                for b in range(NB):
                    P.op("dve", lambda e, ts_=ts_, b=b, g=g, bs_=bs_: e.tensor_tensor(
                        ts_[:, b * 128:(b + 1) * 128], C.ps[:, bs_, b * 128:(b + 1) * 128], bsb[:, g * 128:(g + 1) * 128], ALU.add),
                        reads=[("ps", bs_), "bsb"], writes=[tk])
                P.op("dve", lambda e, ts_=ts_, ug=ug, j=j: e.tensor_tensor(gT[:, j, :], ts_[:], ug[:], ALU.mult),
                     reads=[tk, ugk], writes=[("o_gT", j)])
            for _ in range(bg_steps):
                next(bg, None)
        greads = [("o_gT", j) for j in range(KC)]
        pre_gen = None
        if ti + 1 < len(tiles):
            pre_gen = prenorm_gen(C, R, tiles[ti + 1], "odd_pre_g", hTs[(ti + 1) % 2], ("o_hT", (ti + 1) % 2))
            next(pre_gen, None)
        n = 0
        for g2 in range(D // 256):
            if pre_gen is not None and g2 >= 1:
                next(pre_gen, None)
            ob, ok = ws_.load(w_out, g2 * 256, slab=16 + g2, first=(ti == 0))
            for jj in range(2):
                c = g2 * 2 + jj
                bank = 2 + n % 2
                n += 1
                for k in range(KC):
                    P.op("pe", lambda e, ob=ob, k=k, jj=jj, bank=bank: e.matmul(
                        C.ps[:, bank, :], lhsT=ob[:, k, jj * 128:(jj + 1) * 128], rhs=gT[:, k, :],
                        start=(k == 0), stop=(k == KC - 1)),
                        reads=[ok] + (greads if k in (0, KC - 1) else []), writes=[("ps", bank)])
                P.op("act", lambda e, c=c, bank=bank: e.copy(YT[:, c, :], C.ps[:, bank, :]),
                     reads=[("ps", bank)], writes=[bk(c)])
        if pre_gen is not None:
            for _ in pre_gen:
                pass
        ws_.flush()
        order = [b * 4 + q for q in range(4) for b in range(NB)]
        gen = postnorm_gen(C, R, t, YT, bk, "odd_post_g", 1.0, order=order)
        for _ in range(5):
            next(gen, None)
        if ti == len(tiles) - 1:
            for _ in gen:
                pass
        else:
            R.pending_post = gen

    for ti_, t_ in enumerate(tiles):
        tile_body(ti_, t_)
    for _ in bg:
        pass


def stage_even(C, tiles):
    nc, P = C.nc, C.P
    R = Ctx()
    norm_bufs(C, R, nxt=1)
    NB = T // 128
    SCALE = float((128 + 64) ** -0.5)
    A = C.sb("e_A", [128, KC * T])
    Ab = A.bitcast(BF16)
    YT = A[:].rearrange("p (k t) -> p k t", k=KC)
    hT = Ab[:, 0:KC * T].rearrange("p (k t) -> p k t", k=KC)
    qn = Ab[:, KC * T:KC * T + 8 * T].rearrange("p (k t) -> p k t", k=8)
    qp = Ab[0:64, KC * T + 8 * T:KC * T + 16 * T].rearrange("p (k t) -> p k t", k=8)
    B = C.sb("e_B", [128, KC * T], BF16)
    wuq = B[:, 0:4 * 1536].rearrange("p (k f) -> p k f", k=4)
    wukv = B[:, 0:4 * 2048].rearrange("p (k f) -> p k f", k=4)
    aT = B[:].rearrange("p (k t) -> p k t", k=KC)
    Bkeys = [("e_B", i) for i in range(KC)]
    clat = C.sb("e_clat", [128, 4, T])
    clatn = C.sb("e_clatn", [128, 4, T], BF16)
    wuq_sw = C.sb("e_wuqsw", [128, 4, 8, 64], BF16)
    wkr = C.sb("e_wkr", [128, KC, 128], BF16)
    knT = C.sb("e_knT", [128, 8, SEQ], BF16)
    kpT = C.sb("e_kpT", [64, SEQ], BF16)
    vtok = C.sb("e_vtok", [128, SEQ // 128, 1024], BF16)
    halo = C.sb("e_halo", [128, 8, 32], BF16)
    zc = Rot("e_zc", [C.sb(f"e_zc{i}", [128, 32 + T], BF16) for i in range(2)])
    dg = C.sb("e_dg", [128, 31, 128], BF16)
    acc = Rot("e_acc", [C.sb(f"e_acc{i}", [128, T]) for i in range(2)])
    sg = Rot("e_sg", [C.sb(f"e_sg{i}", [128, T]) for i in range(1)])
    tmp = Rot("e_tmp", [C.sb(f"e_tmp{i}", [128, T]) for i in range(2)])
    pT = Rot("e_pT", [C.sb(f"e_pT{i}", [128, T], BF16) for i in range(2)])
    cs = C.sb("e_cos", [64, T])
    angb = C.sb("e_ang", [128, T])
    sn = C.sb("e_sin", [64, T])
    mask = C.sb("e_mask", [128, 4, T], BF16)
    ws_ = WStream(C, "e_w", KC, 256, 2, n_slabs=28)
    w_in, w_out = C.w["even_w_in"], C.w["even_w_out"]
    w_uq, w_ukv = C.w["even_w_uq"], C.w["even_w_ukv"]
    OFF_CONV = 1088
    todo = upcoming_ffns(C, "even", 2)
    bg = chain_gens(*[convert_gen(C, ab_, l_) for ab_, l_ in todo])
    bg_steps = -(-60 * len(todo) // (8 * max(1, len(tiles))))
    icol = COLS["inv_freq"][0]
    scol = COLS["sin_sign"][0]

    uqv = w_uq.rearrange("(k p) (h c) -> p k h c", p=128, c=192)
    for k in range(4):
        P.op("pool", lambda e, k=k: e.dma_start(out=wuq_sw[:, k, :, 0:32], in_=uqv[:, k, :, 160:192]), writes=[("wuqsw0", k)], dma=True)
        P.op("pool", lambda e, k=k: e.dma_start(out=wuq_sw[:, k, :, 32:64], in_=uqv[:, k, :, 128:160]), writes=[("wuqsw1", k)], dma=True)
    winv = w_in.rearrange("(k p) f -> p k f", p=128)
    P.op("pool", lambda e: e.dma_start(out=wkr[:, :, 0:64], in_=winv[:, :, 1024:1088]), writes=["wkr0"], dma=True)
    P.op("pool", lambda e: e.dma_start(out=wkr[:, :, 64:96], in_=winv[:, :, 1056:1088]), writes=["wkr1"], dma=True)
    P.op("pool", lambda e: e.dma_start(out=wkr[:, :, 96:128], in_=winv[:, :, 1024:1056]), writes=["wkr2"], dma=True)
    mf, mfk = tmp.next()
    for d in range(4):
        P.op("pool", lambda e: e.memset(mf[:], -30000.0), writes=[mfk])
        P.op("pool", lambda e, d=d: e.affine_select(out=mf[:], in_=mf[:], pattern=[[-1, T]], compare_op=ALU.is_ge,
                                                    fill=0.0, base=d * 128 - 1, channel_multiplier=1),
             reads=[mfk], writes=[mfk])
        P.op("pool", lambda e, d=d: e.tensor_copy(mask[:, d, :], mf[:]), reads=[mfk], writes=[("mask", d)])
    P.barrier()
    setup_keys = {}

    def tile_body(ti, t):
        if ti > 0:
            P.barrier()
        ts = t % (SEQ // T)
        s0 = ts * T
        if ts == 0:
            P.op("dve", lambda e: e.memset(halo[:], 0.0), writes=[("halo", c) for c in range(8)])
        prenorm_tile(C, R, t, "even_pre_g", hT, "e_hT")
        hreads = [("e_hT", k, b) for k in range(KC) for b in range(NB)]
        posi, pk_ = tmp.next()
        ang, ak_ = angb, "e_ang"
        posi_i = posi.bitcast(I32)
        P.op("sp", lambda e, t=t, posi_i=posi_i: e.dma_start(out=posi_i[0:64, :], in_=C.pos[:, t * T:(t + 1) * T].partition_broadcast(64)),
             writes=[pk_], dma=True)
        P.op("dve", lambda e, posi_i=posi_i, ang=ang: e.tensor_copy(ang[0:64, :], posi_i[0:64, :]), reads=[pk_], writes=[ak_])
        P.op("dve", lambda e, ang=ang: e.tensor_scalar(ang[0:64, :], ang[0:64, :], C.cols[0:64, icol:icol + 1], None, ALU.mult),
             reads=[ak_, "cols"], writes=[ak_])
        C1 = 6.28125
        C2 = float(2 * np.pi - 6.28125)
        INV2PI = float(1.0 / (2 * np.pi))
        for dst, dkey, off in ((sn, "sn", 0.0), (cs, "cs", float(np.pi / 2))):
            kb_, kk_ = tmp.next()
            kbi = kb_.bitcast(I32)
            P.op("dve", lambda e, kbi=kbi, ang=ang, off=off: e.tensor_scalar(kbi[0:64, :], ang[0:64, :], INV2PI, off * INV2PI, ALU.mult, ALU.add),
                 reads=[ak_], writes=[kk_])
            P.op("dve", lambda e, kbi=kbi, dst=dst: e.tensor_copy(dst[:], kbi[0:64, :]), reads=[kk_], writes=[dkey])
            P.op("dve", lambda e, kb_=kb_, dst=dst, ang=ang: e.scalar_tensor_tensor(out=kb_[0:64, :], in0=dst[:], scalar=-C1, in1=ang[0:64, :],
                                                                                    op0=ALU.mult, op1=ALU.add),
                 reads=[dkey, ak_, kk_], writes=[kk_])
            P.op("dve", lambda e, kb_=kb_, dst=dst: e.scalar_tensor_tensor(out=kb_[0:64, :], in0=dst[:], scalar=-C2, in1=kb_[0:64, :],
                                                                           op0=ALU.mult, op1=ALU.add),
                 reads=[dkey, kk_], writes=[kk_])
            P.op("dve", lambda e, kb_=kb_, dst=dst, off=off: e.tensor_scalar(dst[:], kb_[0:64, :], -1.0, float(np.pi) - off, ALU.mult, ALU.add),
                 reads=[kk_, dkey], writes=[dkey])
            P.op("dve", lambda e, kb_=kb_, off=off: e.tensor_scalar(kb_[0:64, :], kb_[0:64, :], off, None, ALU.add),
                 reads=[kk_], writes=[kk_])
            P.op("dve", lambda e, kb_=kb_, dst=dst: e.tensor_tensor(dst[:], dst[:], kb_[0:64, :], ALU.min),
                 reads=[kk_, dkey], writes=[dkey])
            P.op("dve", lambda e, dst=dst: e.tensor_scalar(dst[:], dst[:], float(-np.pi), float(np.pi), ALU.max, ALU.min),
                 reads=[dkey], writes=[dkey])
            P.op("act", lambda e, dst=dst: e.activation(out=dst[:], in_=dst[:], func=AF.Sin), reads=[dkey], writes=[dkey])
        P.op("dve", lambda e: e.tensor_scalar(sn[:], sn[:], C.cols[0:64, scol:scol + 1], None, ALU.mult),
             reads=["sn", "cols"], writes=["sn"])

        def rope_evac(ps_x, ps_sw, out_ap, okeys, banks):
            t1, k1 = tmp.next()
            t2, k2 = tmp.next()
            P.op("dve", lambda e: e.tensor_tensor(t1[0:64, :], ps_x, cs[:], ALU.mult),
                 reads=[("ps", banks[0]), "cs"], writes=[k1])
            P.op("dve", lambda e: e.tensor_tensor(t2[0:64, :], ps_sw, sn[:], ALU.mult),
                 reads=[("ps", banks[1]), "sn"], writes=[k2])
            P.op("dve", lambda e: e.tensor_tensor(out_ap, t1[0:64, :], t2[0:64, :], ALU.add),
                 reads=[k1, k2], writes=okeys)

        def latent(col0, gname):
            n = 0
            for g in range(2):
                wb, wk = ws_.load(w_in, col0 + g * 256, slab=col0 // 256 + g, first=(ti == 0))
                for jj in range(2):
                    j = g * 2 + jj
                    bank = n % 2
                    n += 1
                    for k in range(KC):
                        P.op("pe", lambda e, wb=wb, k=k, jj=jj, bank=bank: e.matmul(
                            C.ps[:, bank, :], lhsT=wb[:, k, jj * 128:(jj + 1) * 128], rhs=hT[:, k, :],
                            start=(k == 0), stop=(k == KC - 1)),
                            reads=[wk] + (hreads if k in (0, KC - 1) else []), writes=[("ps", bank)])
                    P.op("act", lambda e, j=j, bank=bank: e.copy(clat[:, j, :], C.ps[:, bank, :]),
                         reads=[("ps", bank)], writes=[("clat", j)])
            bank = 6
            for c in range(4):
                sq, sk = R.sq2.next()
                P.op("dve", lambda e, sq=sq, c=c: e.tensor_tensor(sq[:], clat[:, c, :], clat[:, c, :], ALU.mult),
                     reads=[("clat", c)], writes=[sk])
                P.op("pe", lambda e, sq=sq, c=c, bank=bank: e.matmul(C.ps[:, bank, :], lhsT=C.ones_b[:], rhs=sq[:],
                                                                      start=(c == 0), stop=(c == 3)),
                     reads=[sk, "ones_b"], writes=[("ps", bank)])
            rs, rk = R.rs2.next()
            rstd_from_sum(C, C.ps[:, bank, :], T, 512, rs, ("ps", bank), rk)
            for c in range(4):
                P.op("dve", lambda e, c=c, rs=rs: e.scalar_tensor_tensor(
                    out=clatn[:, c, :], in0=clat[:, c, :], scalar=col(C, gname, c), in1=rs[:, 0:T], op0=ALU.mult, op1=ALU.mult),
                    reads=[("clat", c), rk, "cols"], writes=[("clatn", c)])

        lreads = [("clatn", c) for c in range(4)]
        P.op("pool", lambda e: e.dma_start(out=wuq, in_=w_uq.rearrange("(k p) f -> p k f", p=128)), writes=Bkeys[0:12], dma=True)
        latent(0, "q_norm_g")
        for h in range(8):
            bank = 2 + h % 2
            for k in range(4):
                P.op("pe", lambda e, h=h, k=k, bank=bank: e.matmul(
                    C.ps[:, bank, :], lhsT=wuq[:, k, h * 192:h * 192 + 128], rhs=clatn[:, k, :],
                    start=(k == 0), stop=(k == 3)),
                    reads=Bkeys[0:12] + lreads, writes=[("ps", bank)])
            P.op("act", lambda e, h=h, bank=bank: e.copy(qn[:, h, :], C.ps[:, bank, :]),
                 reads=[("ps", bank)], writes=[("qn", h)])
            b0, b1 = (4, 5) if h % 2 == 0 else (0, 1)
            for k in range(4):
                P.op("pe", lambda e, h=h, k=k, b0=b0: e.matmul(
                    C.ps[0:64, b0, :], lhsT=wuq[:, k, h * 192 + 128:h * 192 + 192], rhs=clatn[:, k, :],
                    start=(k == 0), stop=(k == 3)),
                    reads=Bkeys[0:12] + lreads, writes=[("ps", b0)])
            for k in range(4):
                P.op("pe", lambda e, h=h, k=k, b1=b1: e.matmul(
                    C.ps[0:64, b1, :], lhsT=wuq_sw[:, k, h, :], rhs=clatn[:, k, :],
                    start=(k == 0), stop=(k == 3)),
                    reads=lreads, writes=[("ps", b1)])
            rope_evac(C.ps[0:64, b0, :], C.ps[0:64, b1, :], qp[:, h, :], [("qp", h)], (b0, b1))
        P.op("pool", lambda e: e.dma_start(out=wukv, in_=w_ukv.rearrange("(k p) f -> p k f", p=128)), writes=Bkeys, dma=True)
        latent(512, "kv_norm_g")
        for h in range(8):
            bank = 2 + h % 2
            for k in range(4):
                P.op("pe", lambda e, h=h, k=k, bank=bank: e.matmul(
                    C.ps[:, bank, :], lhsT=wukv[:, k, h * 256:h * 256 + 128], rhs=clatn[:, k, :],
                    start=(k == 0), stop=(k == 3)),
                    reads=Bkeys + lreads, writes=[("ps", bank)])
            P.op("act", lambda e, h=h, bank=bank: e.copy(knT[:, h, s0:s0 + T], C.ps[:, bank, :]),
                 reads=[("ps", bank)], writes=[("knT", h)])
        wukv_v = wukv.rearrange("p k (h c) -> p k h c", c=256)
        for b in range(NB):
            for half in range(2):
                bank = 4 + (b * 2 + half) % 2
                for k in range(4):
                    P.op("pe", lambda e, b=b, k=k, half=half, bank=bank: e.matmul(
                        C.ps[:, bank, :].rearrange("p (h c) -> p h c", c=128),
                        lhsT=clatn[:, k, b * 128:(b + 1) * 128], rhs=wukv_v[:, k, half * 4:(half + 1) * 4, 128:256],
                        start=(k == 0), stop=(k == 3)),
                        reads=Bkeys + lreads, writes=[("ps", bank)])
                P.op("dve", lambda e, b=b, half=half, bank=bank: e.tensor_copy(
                    vtok[:, ts * NB + b, half * 512:(half + 1) * 512], C.ps[:, bank, :]),
                    reads=[("ps", bank)], writes=["vtok"])
        for k in range(KC):
            P.op("pe", lambda e, k=k: e.matmul(C.ps[0:64, 0, :], lhsT=wkr[:, k, 0:64], rhs=hT[:, k, :],
                                               start=(k == 0), stop=(k == KC - 1)),
                 reads=(hreads if k in (0, KC - 1) else []), writes=[("ps", 0)])
        for k in range(KC):
            P.op("pe", lambda e, k=k: e.matmul(C.ps[0:64, 1, :], lhsT=wkr[:, k, 64:128], rhs=hT[:, k, :],
                                               start=(k == 0), stop=(k == KC - 1)),
                 reads=(hreads if k in (0, KC - 1) else []), writes=[("ps", 1)])
        rope_evac(C.ps[0:64, 0, :], C.ps[0:64, 1, :], kpT[:, s0:s0 + T], ["kpT"], (0, 1))
        off_w = COLS["conv_w"][0]
        nkb = (ts + 1) * NB

        zst = {}

        def conv_proj(c):
            gb_, gk_ = ws_.load(w_in, OFF_CONV + 1024 + c * 128, 128, slab=4 + 2 * c, first=(ti == 0))
            for k in range(KC):
                P.op("pe", lambda e, gb_=gb_, k=k: e.matmul(C.ps[:, 2, :], lhsT=gb_[:, k, 0:128], rhs=hT[:, k, :],
                                                            start=(k == 0), stop=(k == KC - 1)),
                     reads=[gk_] + (hreads if k in (0, KC - 1) else []), writes=[("ps", 2)])
            sgt, sgk = sg.next()
            P.op("act", lambda e, sgt=sgt: e.activation(out=sgt[:], in_=C.ps[:, 2, :], func=AF.Sigmoid),
                 reads=[("ps", 2)], writes=[sgk])
            ab_, ak2_ = ws_.load(w_in, OFF_CONV + c * 128, 128, slab=5 + 2 * c, first=(ti == 0))
            for k in range(KC):
                P.op("pe", lambda e, ab_=ab_, k=k: e.matmul(C.ps[:, 3, :], lhsT=ab_[:, k, 0:128], rhs=hT[:, k, :],
                                                            start=(k == 0), stop=(k == KC - 1)),
                     reads=[ak2_] + (hreads if k in (0, KC - 1) else []), writes=[("ps", 3)])
            z, zk = zc.next()
            zst[c] = (z, zk)
            P.op("dve", lambda e, z=z, c=c: e.tensor_copy(z[:, 2:32], halo[:, c, 2:32]), reads=[("halo", c)], writes=[zk])
            P.op("dve", lambda e, z=z, sgt=sgt: e.tensor_tensor(z[:, 32:32 + T], C.ps[:, 3, :], sgt[:], ALU.mult),
                 reads=[("ps", 3), sgk, zk], writes=[zk])
            P.op("dve", lambda e, c=c, z=z: e.tensor_copy(halo[:, c, 2:32], z[:, T + 2:T + 32]), reads=[zk], writes=[("halo", c)])
            P.op("dve", lambda e, c=c: e.tensor_tensor(
                dg[:], bass.AP(C.ident_f, 0, [[128, 128], [0, 31], [1, 128]]),
                bass.AP(C.cols, off_w + c * 31, [[NCOL, 128], [1, 31], [0, 128]]), ALU.mult),
                reads=["ident_f", "cols"], writes=["dg"])

        def conv_mm(c):
            z, zk = zst[c]
            for j in range(31):
                P.op("pe", lambda e, z=z, j=j: e.matmul(C.ps[:, 2, :], lhsT=dg[:, j, :], rhs=z[:, 2 + j:2 + j + T],
                                                        start=(j == 0), stop=(j == 30)),
                     reads=["dg", zk], writes=[("ps", 2)])
            at, akey = acc.next()
            zst[c] = (at, akey)
            P.op("act", lambda e, at=at, c=c: e.activation(out=at[:], in_=C.ps[:, 2, :], func=AF.Identity, bias=col(C, "conv_b", c)),
                 reads=[("ps", 2), "cols"], writes=[akey])

        def conv_ln(c):
            at, akey = zst[c]
            atb, atbk = R.sq2.next()
            P.op("act", lambda e, at=at, atb=atb: e.copy(atb[:], at[:]), reads=[akey], writes=[atbk])
            P.op("pe", lambda e, atb=atb: e.matmul(C.ps[:, 3, :], lhsT=C.ones_b[:], rhs=atb[:], start=True, stop=True),
                 reads=[atbk, "ones_b"], writes=[("ps", 3)])
            P.op("dve", lambda e, at=at: e.scalar_tensor_tensor(out=at[:], in0=C.ps[:, 3, :], scalar=-1.0 / 128, in1=at[:],
                                                                 op0=ALU.mult, op1=ALU.add),
                 reads=[("ps", 3), akey], writes=[akey])
            sq, sk = R.sq2.next()
            P.op("dve", lambda e, sq=sq, at=at: e.tensor_tensor(sq[:], at[:], at[:], ALU.mult), reads=[akey], writes=[sk])
            P.op("pe", lambda e, sq=sq: e.matmul(C.ps[:, 3, :], lhsT=C.ones_b[:], rhs=sq[:], start=True, stop=True),
                 reads=[sk, "ones_b"], writes=[("ps", 3)])
            rs, rk = R.rs2.next()
            rstd_from_sum(C, C.ps[:, 3, :], T, 128, rs, ("ps", 3), rk)
            P.op("dve", lambda e, at=at, rs=rs: e.tensor_tensor(at[:], at[:], rs[:, 0:T], ALU.mult), reads=[akey, rk], writes=[akey])
            P.op("act", lambda e, at=at, c=c: e.activation(out=aT[:, 8 + c, :], in_=at[:], func=AF.Silu,
                                                           scale=col(C, "conv_ng", c), bias=col(C, "conv_nb", c)),
                 reads=[akey, "cols"], writes=[Bkeys[8 + c]])

        def attn_head(h):
            bo, br = 4 + (h % 2) * 2, 5 + (h % 2) * 2

            def s_mm(kb):
                bs_ = kb % 2
                dgn = kb - ts * NB
                P.op("pe", lambda e, h=h, kb=kb, bs_=bs_: e.matmul(
                    C.ps[:, bs_, :], lhsT=knT[:, h, kb * 128:(kb + 1) * 128], rhs=qn[:, h, :], start=True, stop=False),
                    reads=[("knT", h), ("qn", h)], writes=[("ps", bs_)])
                P.op("pe", lambda e, h=h, kb=kb, bs_=bs_, dgn=dgn: e.matmul(
                    C.ps[:, bs_, :], lhsT=kpT[:, kb * 128:(kb + 1) * 128], rhs=qp[:, h, :], start=False, stop=(dgn < 0)),
                    reads=["kpT", ("qp", h)], writes=[("ps", bs_)])
                if dgn >= 0:
                    P.op("pe", lambda e, bs_=bs_, dgn=dgn: e.matmul(
                        C.ps[:, bs_, :], lhsT=C.ident_b[:], rhs=mask[:, dgn, :], start=False, stop=True),
                        reads=["ident_b", ("mask", dgn)], writes=[("ps", bs_)])

            s_mm(0)
            for kb in range(nkb):
                bs_ = kb % 2
                pt, pk = pT.next()
                P.op("act", lambda e, pt=pt, bs_=bs_: e.activation(out=pt[:], in_=C.ps[:, bs_, :], func=AF.Exp, scale=SCALE),
                     reads=[("ps", bs_)], writes=[pk])
                if kb + 1 < nkb:
                    s_mm(kb + 1)
                P.op("pe", lambda e, h=h, kb=kb, pt=pt, bo=bo: e.matmul(
                    C.ps[:, bo, :], lhsT=vtok[:, kb, h * 128:(h + 1) * 128], rhs=pt[:], start=(kb == 0), stop=(kb == nkb - 1)),
                    reads=["vtok", pk], writes=[("ps", bo)])
                P.op("pe", lambda e, pt=pt, br=br, kb=kb: e.matmul(
                    C.ps[:, br, :], lhsT=C.ones_b[:], rhs=pt[:], start=(kb == 0), stop=(kb == nkb - 1)),
                    reads=["ones_b", pk], writes=[("ps", br)])
            rt, rk = tmp.next()
            P.op("dve", lambda e, rt=rt, br=br: e.reciprocal(rt[:], C.ps[:, br, :]), reads=[("ps", br)], writes=[rk])
            P.op("dve", lambda e, rt=rt, bo=bo, h=h: e.tensor_tensor(aT[:, h, :], C.ps[:, bo, :], rt[:], ALU.mult),
                 reads=[("ps", bo), rk], writes=[Bkeys[h]])

        conv_proj(0)
        for c in range(8):
            conv_mm(c)
            if c + 1 < 8:
                conv_proj(c + 1)
            attn_head(c)
            conv_ln(c)
            for _ in range(bg_steps):
                next(bg, None)
        n = 0
        for g in range(D // 256):
            ob, ok = ws_.load(w_out, g * 256, slab=20 + g, first=(ti == 0))
            for jj in range(2):
                c = g * 2 + jj
                bank = 2 + n % 2
                n += 1
                for k in range(KC):
                    P.op("pe", lambda e, ob=ob, k=k, jj=jj, bank=bank: e.matmul(
                        C.ps[:, bank, :], lhsT=ob[:, k, jj * 128:(jj + 1) * 128], rhs=aT[:, k, :],
                        start=(k == 0), stop=(k == KC - 1)),
                        reads=[ok] + (Bkeys if k in (0, KC - 1) else []), writes=[("ps", bank)])
                P.op("act", lambda e, c=c, bank=bank: e.copy(YT[:, c, :], C.ps[:, bank, :]),
                     reads=[("ps", bank)], writes=[("e_YT", c)])
        ws_.flush()
        postnorm_residual_tile(C, R, t, YT, lambda c: ("e_YT", c), "even_post_g", 1.0, accum=ACCUM_EVEN)

    for ti_, t_ in enumerate(tiles):
        tile_body(ti_, t_)
    for _ in bg:
        pass


ALL_STAGES = ["tin", "ffn_a0", "even", "ffn_b0", "ffn_a1", "odd", "ffn_b1"]


def colsify(v):
    v = np.asarray(v, np.float32).reshape(-1, 128)
    return np.ascontiguousarray(v.T)


def make_cols(inp):
    cols = np.zeros((128, NCOL), np.float32)

    def put(name, arr):
        off, w = COLS[name]
        assert arr.shape == (128, w), (name, arr.shape, w)
        cols[:, off:off + w] = arr

    for l in range(2):
        for ab in "ab":
            put(f"ffn_{ab}_pre_g{l}", colsify(inp[f"ffn_{ab}_pre_g"][l]))
            put(f"ffn_{ab}_post_g{l}", colsify(inp[f"ffn_{ab}_post_g"][l]))
    put("even_pre_g", colsify(inp["even_pre_g"][0]))
    put("even_post_g", colsify(inp["even_post_g"][0]))
    put("odd_pre_g", colsify(inp["odd_pre_g"][0]))
    put("odd_post_g", colsify(inp["odd_post_g"][0]))
    put("q_norm_g", colsify(inp["even_q_norm_g"][0]))
    put("kv_norm_g", colsify(inp["even_kv_norm_g"][0]))
    cw = np.asarray(inp["even_conv_w"][0], np.float32)
    put("conv_w", np.ascontiguousarray(cw.reshape(31, 8, 128).transpose(2, 1, 0).reshape(128, 248)))
    put("conv_b", colsify(inp["even_conv_b"][0]))
    put("conv_ng", colsify(inp["even_conv_norm_g"][0]))
    put("conv_nb", colsify(inp["even_conv_norm_b"][0]))
    inv_freq = (np.float32(10000.0) ** (-np.arange(0, 64, 2, dtype=np.float32) / np.float32(64))).astype(np.float32)
    c = np.zeros((128, 1), np.float32)
    c[0:64, 0] = np.concatenate([inv_freq, inv_freq])
    put("inv_freq", c)
    s = np.zeros((128, 1), np.float32)
    s[0:32, 0] = -1.0
    s[32:64, 0] = 1.0
    put("sin_sign", s)
    return cols


def make_in_maps(inp, n_cores=N_CORES):
    shared = {"cols": make_cols(inp)}
    for l in range(2):
        for ab in "ab":
            for n in ("gate", "up", "down"):
                shared[f"ffn_{ab}_w_{n}{l}"] = np.ascontiguousarray(inp[f"ffn_{ab}_w_{n}"][l])
    for n in ("even_w_in", "even_w_uq", "even_w_ukv", "even_w_out", "odd_w_in", "odd_w_out"):
        shared[n] = np.ascontiguousarray(inp[n][0])
    shared["odd_v_norm_g"] = np.ascontiguousarray(inp["odd_v_norm_g"][0]).reshape(1, D)
    shared["odd_v_norm_b"] = np.ascontiguousarray(inp["odd_v_norm_b"][0]).reshape(1, D)
    shared["odd_w_s"] = np.ascontiguousarray(inp["odd_w_s"][0])
    shared["odd_b_s"] = np.ascontiguousarray(inp["odd_b_s"][0]).reshape(1, 8 * 128)
    x = np.asarray(inp["x"])
    pos = np.asarray(inp["positions"])
    maps = []
    for c in range(n_cores):
        m = dict(shared)
        m["x"] = np.ascontiguousarray(x[2 * c:2 * c + 2].reshape(NTOK, D))
        m["pos"] = np.ascontiguousarray(pos[2 * c:2 * c + 2].reshape(1, NTOK)).astype(np.int32)
        maps.append(m)
    return maps


_CACHE = {}


def kernel(**inputs):
    inputs = {k: np.asarray(v) for k, v in inputs.items()}
    if "nc" not in _CACHE:
        _CACHE["nc"] = build_program(ALL_STAGES, list(range(NTOK // T)))[0]
    nc = _CACHE["nc"]
    maps = make_in_maps(inputs)
    res = run_bass_kernel_spmd(nc, maps, core_ids=list(range(N_CORES)))
    out = np.stack([r["y"].reshape(2, SEQ, D) for r in res.results], axis=0).reshape(16, SEQ, D)
    return out.astype(np.float32, copy=False)
```

```python
import os
import numpy as np
import concourse.bass as bass
import concourse.mybir as mybir
from concourse.bass_utils import run_bass_kernel_spmd

F32 = mybir.dt.float32
BF16 = mybir.dt.bfloat16
I32 = mybir.dt.int32
ALU = mybir.AluOpType
AF = mybir.ActivationFunctionType
AX = mybir.AxisListType

D = 2048
DFF = 5632
SEQ = 2048
NTOK = 4096
T = 512
KC = D // 128
FC = DFF // 128
EPS = 1e-6
IN_EVEN = 3136
N_CORES = 8

ENGS = ("pe", "act", "dve", "pool", "sp")
SEM_ROLL = 20000
PIPE_POST = int(os.environ.get('K_PIPE_POST', '1'))
PIPE_PRE = int(os.environ.get('K_PIPE_PRE', '1'))
ACCUM_EVEN = int(os.environ.get('K_ACCUM_EVEN', '0'))


class Op:
    __slots__ = ("eng", "fn", "reads", "writes", "dma", "deps", "signal", "sem", "val")

    def __init__(self, eng, fn, reads, writes, dma):
        self.eng = eng
        self.fn = fn
        self.reads = reads
        self.writes = writes
        self.dma = dma
        self.deps = []
        self.signal = False
        self.sem = None
        self.val = 0


class Prog:
    def __init__(self, nc, n_dma_sems=10, same_engine_sync=True):
        self.nc = nc
        self.ops = []
        self.last_writer = {}
        self.readers = {}
        self.n_dma_sems = n_dma_sems
        self.same_engine_sync = same_engine_sync
        self.last_ops = {e: None for e in ENGS}
        self.recent_dma = {e: [] for e in ENGS}
        self.pending_bar = {e: [] for e in ENGS}

    def op(self, eng, fn, reads=(), writes=(), dma=False):
        o = Op(eng, fn, tuple(reads), tuple(writes), dma)
        deps = set()
        for k in o.reads:
            w = self.last_writer.get(k)
            if w is not None:
                deps.add(w)
        for k in o.writes:
            w = self.last_writer.get(k)
            if w is not None:
                deps.add(w)
            for r in self.readers.get(k, ()):
                deps.add(r)
        for k in o.writes:
            self.last_writer[k] = o
            self.readers[k] = []
        for k in o.reads:
            self.readers.setdefault(k, []).append(o)
        deps.discard(o)
        for d in deps:
            if d.eng == o.eng and not d.dma and not o.dma:
                if o.eng == "pe" or not self.same_engine_sync:
                    continue
            o.deps.append(d)
            d.signal = True
        if self.pending_bar[eng]:
            for d in self.pending_bar[eng]:
                if d.eng == eng and not d.dma:
                    continue
                o.deps.append(d)
                d.signal = True
            self.pending_bar[eng] = []
        self.ops.append(o)
        if dma:
            self.recent_dma[eng].append(o)
            if len(self.recent_dma[eng]) > self.n_dma_sems:
                self.recent_dma[eng].pop(0)
        else:
            self.last_ops[eng] = o
        return o

    def barrier(self):
        bar = []
        for e in ENGS:
            if self.last_ops[e] is not None:
                bar.append(self.last_ops[e])
            bar.extend(self.recent_dma[e])
        for e in ENGS:
            self.pending_bar[e] = list(bar)
        self.last_writer = {}
        self.readers = {}

    def emit(self):
        nc = self.nc
        per_eng = {e: [] for e in ENGS}
        for o in self.ops:
            per_eng[o.eng].append(o)

        dma_prev = {}
        for e in ENGS:
            cur = None
            cnt = 0
            nroll = 0
            pool_sems = None
            ndma = 0
            for o in per_eng[e]:
                if o.dma:
                    if pool_sems is None:
                        pool_sems = [nc.alloc_semaphore(name=f"d_{e}_{i}") for i in range(self.n_dma_sems)]
                    s = pool_sems[ndma % self.n_dma_sems]
                    rnd = ndma // self.n_dma_sems
                    o.sem = s
                    o.val = 16 * (rnd + 1)
                    if rnd > 0:
                        dma_prev[o] = (s, 16 * rnd)
                    ndma += 1
                elif o.signal:
                    if cur is None or cnt >= SEM_ROLL:
                        cur = nc.alloc_semaphore(name=f"c_{e}_{nroll}")
                        nroll += 1
                        cnt = 0
                    cnt += 1
                    o.sem = cur
                    o.val = cnt
        final_waits = []
        for e in ENGS:
            last = {}
            for o in per_eng[e]:
                if o.dma:
                    last[id(o.sem)] = o
            final_waits.extend(last.values())
        stats = {"waits": 0, "incs": 0, "ops": len(self.ops)}

        def run_engine(e, eng):
            known = {}
            for o in per_eng[e]:
                need = {}
                if o in dma_prev:
                    s, v = dma_prev[o]
                    need[id(s)] = (s, v)
                for d in o.deps:
                    k = id(d.sem)
                    if k not in need or need[k][1] < d.val:
                        need[k] = (d.sem, d.val)
                for k, (s, v) in need.items():
                    if known.get(k, 0) >= v:
                        continue
                    eng.wait_ge(s, v)
                    known[k] = v
                    stats["waits"] += 1
                ins = o.fn(eng)
                if o.dma:
                    ins.then_inc(o.sem, 16)
                elif o.signal:
                    ins.then_inc(o.sem, 1)
                    stats["incs"] += 1
            if e == "sp":
                for o in final_waits:
                    if known.get(id(o.sem), 0) >= o.val:
                        continue
                    eng.wait_ge(o.sem, o.val)

        with nc.Block() as block:
            @block.tensor
            def _(eng):
                run_engine("pe", eng)

            @block.scalar
            def _(eng):
                run_engine("act", eng)

            @block.vector
            def _(eng):
                run_engine("dve", eng)

            @block.gpsimd
            def _(eng):
                run_engine("pool", eng)

            @block.sync
            def _(eng):
                run_engine("sp", eng)
        return stats


COLS = {}
_off = 0
for _n, _w in [("ffn_a_pre_g0", 16), ("ffn_a_post_g0", 16), ("ffn_b_pre_g0", 16), ("ffn_b_post_g0", 16),
               ("ffn_a_pre_g1", 16), ("ffn_a_post_g1", 16), ("ffn_b_pre_g1", 16), ("ffn_b_post_g1", 16),
               ("even_pre_g", 16), ("even_post_g", 16), ("odd_pre_g", 16), ("odd_post_g", 16),
               ("q_norm_g", 4), ("kv_norm_g", 4), ("conv_w", 248), ("conv_b", 8), ("conv_ng", 8),
               ("conv_nb", 8), ("inv_freq", 1), ("sin_sign", 1)]:
    COLS[_n] = (_off, _w)
    _off += _w
NCOL = _off


class Ctx:
    pass


def build_program(stages, tiles, debug_out=None):
    nc = bass.Bass("TRN2", target_bir_lowering=False)
    P = Prog(nc)
    C = Ctx()
    C.nc, C.P = nc, P

    def din(name, shape, dt=F32):
        return nc.dram_tensor(name, list(shape), dt, kind="ExternalInput").ap()

    C.x_in = din("x", [NTOK, D])
    C.pos = din("pos", [1, NTOK], I32)
    C.cols_d = din("cols", [128, NCOL])
    C.w = {}
    for l in range(2):
        for ab in "ab":
            C.w[f"ffn_{ab}_w_gate{l}"] = din(f"ffn_{ab}_w_gate{l}", [D, DFF])
            C.w[f"ffn_{ab}_w_up{l}"] = din(f"ffn_{ab}_w_up{l}", [D, DFF])
            C.w[f"ffn_{ab}_w_down{l}"] = din(f"ffn_{ab}_w_down{l}", [DFF, D])
    C.w["even_w_in"] = din("even_w_in", [D, IN_EVEN])
    C.w["even_w_uq"] = din("even_w_uq", [512, 1536])
    C.w["even_w_ukv"] = din("even_w_ukv", [512, 2048])
    C.w["even_w_out"] = din("even_w_out", [D, D])
    C.w["odd_w_in"] = din("odd_w_in", [D, 2 * D])
    C.w["odd_w_out"] = din("odd_w_out", [D, D])
    C.odd_vg = din("odd_v_norm_g", [1, D])
    C.odd_vb = din("odd_v_norm_b", [1, D])
    C.odd_ws = din("odd_w_s", [8, 128, 128])
    C.odd_bs = din("odd_b_s", [1, 8 * 128])
    C.y_out = nc.dram_tensor("y", [NTOK, D], F32, kind="ExternalOutput").ap()
    C.xT = nc.dram_tensor("xT_scratch", [D, NTOK], F32).ap()
    C.xTv = C.xT.rearrange("(k p) t -> p k t", p=128)

    cnt = [0]

    def sb(name, shape, dt=F32):
        cnt[0] += 1
        return nc.alloc_sbuf_tensor(f"{name}_{cnt[0]}", list(shape), dt)

    C.sb = sb
    C.cols = sb("cols_sb", [128, NCOL])
    C.ones_f = sb("ones_f", [128, 128])
    C.ones_b = sb("ones_b", [128, 128], BF16)
    C.ident_f = sb("ident_f", [128, 128])
    C.ident_b = sb("ident_b", [128, 128], BF16)
    C.ps = nc.alloc_psum_tensor("ps", [128, 8, 512], F32)

    P.op("sp", lambda e: e.dma_start(out=C.cols[:], in_=C.cols_d), writes=["cols"], dma=True)
    P.op("dve", lambda e: e.memset(C.ones_f[:], 1.0), writes=["ones_f"])
    P.op("dve", lambda e: e.memset(C.ones_b[:], 1.0), writes=["ones_b"])
    P.op("pool", lambda e: e.memset(C.ident_f[:], 1.0), writes=["ident_f"])
    P.op("pool", lambda e: e.affine_select(out=C.ident_f[:], in_=C.ident_f[:], pattern=[[-1, 128]],
                                           compare_op=ALU.is_equal, fill=0.0, base=0, channel_multiplier=1),
         reads=["ident_f"], writes=["ident_f"])
    P.op("dve", lambda e: e.tensor_copy(C.ident_b[:], C.ident_f[:]), reads=["ident_f"], writes=["ident_b"])

    C.fcache = {}
    C.fconv_done = set()
    C.fconv_claimed = set()
    for l in range(2):
        for ab in "ab":
            C.fcache[(ab, l)] = {
                "g": nc.dram_tensor(f"wc_g_{ab}{l}", [DFF // 256, 128, KC * 256], BF16).ap(),
                "u": nc.dram_tensor(f"wc_u_{ab}{l}", [DFF // 256, 128, KC * 256], BF16).ap(),
                "d": nc.dram_tensor(f"wc_d_{ab}{l}", [KC, 128, FC * 128], BF16).ap(),
            }
    C.stages = list(stages)
    base0 = nc.sbuf_base
    for st in stages:
        nc.sbuf_base = base0
        P.barrier()
        if st == "tin":
            stage_transpose_in(C, tiles)
        elif st == "tout":
            stage_transpose_out(C, tiles)
        elif st.startswith("ffn"):
            ab, l = st[4], int(st[5])
            stage_ffn(C, tiles, ab, l, final=(st == stages[-1]))
        elif st == "odd":
            stage_odd(C, tiles)
        elif st == "even":
            stage_even(C, tiles)
        else:
            raise ValueError(st)
        print("stage", st, "sbuf bytes left", nc.sbuf_top - nc.sbuf_base)
    C.stats = P.emit()
    return nc, C


def col(C, name, i=0):
    off, w = COLS[name]
    return C.cols[:, off + i:off + i + 1]


def stage_transpose_in(C, tiles):
    nc, P = C.nc, C.P
    xin = [C.sb(f"ti_x{i}", [128, D]) for i in range(4)]
    xo = [C.sb(f"ti_o{i}", [128, KC, 128]) for i in range(4)]
    n = 0
    for t in tiles:
        for b in range(T // 128):
            t0 = t * T + b * 128
            s = n % 4
            P.op("sp", lambda e, s=s, t0=t0: e.dma_start(out=xin[s][:], in_=C.x_in[t0:t0 + 128, :]),
                 writes=[("ti_x", s)], dma=True)
            for g in range(4):
                bank = (n * 4 + g) % 8
                for kk in range(4):
                    k = g * 4 + kk
                    P.op("pe", lambda e, s=s, k=k, bank=bank, kk=kk: e.transpose(
                        C.ps[:, bank, kk * 128:(kk + 1) * 128], xin[s][:, k * 128:(k + 1) * 128], C.ident_f[:]),
                        reads=[("ti_x", s), "ident_f"], writes=[("ps", bank)])
                eng = "act" if g % 2 == 0 else "dve"
                if eng == "act":
                    P.op("act", lambda e, s=s, g=g, bank=bank: e.copy(
                        xo[s][:, g * 4:(g + 1) * 4, :], C.ps[:, bank, :].rearrange("p (a b) -> p a b", a=4)),
                        reads=[("ps", bank)], writes=[("ti_o", s, g)])
                else:
                    P.op("dve", lambda e, s=s, g=g, bank=bank: e.tensor_copy(
                        xo[s][:, g * 4:(g + 1) * 4, :], C.ps[:, bank, :].rearrange("p (a b) -> p a b", a=4)),
                        reads=[("ps", bank)], writes=[("ti_o", s, g)])
            P.op("sp", lambda e, s=s, t0=t0: e.dma_start(out=C.xTv[:, :, t0:t0 + 128], in_=xo[s][:]),
                 reads=[("ti_o", s, g) for g in range(4)], writes=[("xT", t0 // T, c) for c in range(KC)], dma=True)
            n += 1


def stage_transpose_out(C, tiles):
    nc, P = C.nc, C.P
    xin = [C.sb(f"to_x{i}", [128, KC, 128]) for i in range(2)]
    xo = [C.sb(f"to_o{i}", [128, D]) for i in range(2)]
    n = 0
    for t in tiles:
        for b in range(T // 128):
            t0 = t * T + b * 128
            s = n % 2
            P.op("sp", lambda e, s=s, t0=t0: e.dma_start(out=xin[s][:], in_=C.xTv[:, :, t0:t0 + 128]),
                 reads=[("xT", t0 // T, c) for c in range(KC)], writes=[("to_x", s)], dma=True)
            for g in range(4):
                bank = (n * 4 + g) % 8
                for kk in range(4):
                    k = g * 4 + kk
                    P.op("pe", lambda e, s=s, k=k, bank=bank, kk=kk: e.transpose(
                        C.ps[:, bank, kk * 128:(kk + 1) * 128], xin[s][:, k, :], C.ident_f[:]),
                        reads=[("to_x", s), "ident_f"], writes=[("ps", bank)])
                if g % 2 == 0:
                    P.op("act", lambda e, s=s, g=g, bank=bank: e.copy(
                        xo[s][:, g * 512:(g + 1) * 512], C.ps[:, bank, :]),
                        reads=[("ps", bank)], writes=[("to_o", s, g)])
                else:
                    P.op("dve", lambda e, s=s, g=g, bank=bank: e.tensor_copy(
                        xo[s][:, g * 512:(g + 1) * 512], C.ps[:, bank, :]),
                        reads=[("ps", bank)], writes=[("to_o", s, g)])
            P.op("sp", lambda e, s=s, t0=t0: e.dma_start(out=C.y_out[t0:t0 + 128, :], in_=xo[s][:]),
                 reads=[("to_o", s, g) for g in range(4)], writes=[("y", t0)], dma=True)
            n += 1


class Rot:
    def __init__(self, name, bufs):
        self.name, self.bufs, self.i = name, bufs, 0

    def next(self):
        s = self.i % len(self.bufs)
        self.i += 1
        return self.bufs[s], (self.name, s)


def rstd_from_sum(C, ps_ap, width, n_feat, scratch, key_in, key_out, parts=128):
    P = C.P
    P.op("dve", lambda e: e.tensor_scalar(scratch[0:parts, 0:width], ps_ap, 1.0 / n_feat, EPS, ALU.mult, ALU.add),
         reads=[key_in], writes=[key_out])
    P.op("act", lambda e: e.activation(out=scratch[0:parts, 0:width], in_=scratch[0:parts, 0:width], func=AF.Sqrt),
         reads=[key_out], writes=[key_out])
    P.op("dve", lambda e: e.reciprocal(scratch[0:parts, 0:width], scratch[0:parts, 0:width]),
         reads=[key_out], writes=[key_out])


def prenorm_gen(C, R, t, gname, hT, hkey):
    P = C.P
    for b in range(T // 128):
        t0 = t * T + b * 128
        xt, xk = R.xt.next()
        P.op("sp", lambda e, xt=xt, t0=t0: e.dma_start(out=xt[:], in_=C.xTv[:, :, t0:t0 + 128]),
             reads=[("xT", t, c) for c in range(KC)], writes=[xk], dma=True)
        bank = 6
        split = len(R.sq.bufs) >= KC // 4

        def mm(sq, sk, q):
            for kk in range(4):
                k = q * 4 + kk
                P.op("pe", lambda e, sq=sq, k=k, kk=kk, bank=bank: e.matmul(C.ps[:, bank, 0:128], lhsT=C.ones_b[:], rhs=sq[:, kk, :],
                                                                             start=(k == 0), stop=(k == KC - 1)),
                     reads=[sk, "ones_b"], writes=[("ps", bank)])

        sqs = []
        for q in range(KC // 4):
            sq, sk = R.sq.next()
            sqs.append((sq, sk))
            P.op("dve", lambda e, xt=xt, sq=sq, q=q: e.tensor_tensor(sq[:], xt[:, q * 4:(q + 1) * 4, :], xt[:, q * 4:(q + 1) * 4, :], ALU.mult),
                 reads=[xk], writes=[sk])
            if not split:
                mm(sq, sk, q)
        yield
        if split:
            for q in range(KC // 4):
                mm(sqs[q][0], sqs[q][1], q)
        rs, rk = R.rs.next()
        rstd_from_sum(C, C.ps[:, bank, 0:128], 128, D, rs, ("ps", bank), rk)
        for k in range(KC):
            P.op("dve", lambda e, xt=xt, rs=rs, k=k, b=b: e.scalar_tensor_tensor(
                out=hT[:, k, b * 128:(b + 1) * 128], in0=xt[:, k, :], scalar=col(C, gname, k), in1=rs[:, 0:128],
                op0=ALU.mult, op1=ALU.mult),
                reads=[xk, rk, "cols"], writes=[(hkey, k, b)])
        yield


def prenorm_tile(C, R, t, gname, hT, hkey):
    for _ in prenorm_gen(C, R, t, gname, hT, hkey):
        pass


def convert_gen(C, ab, l):
    P = C.P
    ch = C.fcache[(ab, l)]
    for key, wname, kc, gw, nsl in (("g", "gate", KC, 256, DFF // 256), ("u", "up", KC, 256, DFF // 256), ("d", "down", FC, 128, KC)):
        wv = C.w[f"ffn_{ab}_w_{wname}{l}"].rearrange("(k p) f -> p k f", p=128)
        for sl in range(nsl):
            dst = ch[key][sl].rearrange("p (k f) -> p k f", k=kc)
            P.op("pool", lambda e, dst=dst, wv=wv, sl=sl, gw=gw: e.dma_start(out=dst, in_=wv[:, :, sl * gw:(sl + 1) * gw]),
                 writes=[("wc", f"w{key}{ab}{l}", sl)], dma=True)
            yield
    C.fconv_done.add((ab, l))


def chain_gens(*gens):
    for g in gens:
        for _ in g:
            yield


def upcoming_ffns(C, st, n):
    i = C.stages.index(st)
    out = []
    for s2 in C.stages[i + 1:]:
        if s2.startswith("ffn") and (s2[4], int(s2[5])) not in C.fconv_claimed:
            out.append((s2[4], int(s2[5])))
            if len(out) == n:
                break
    for k in out:
        C.fconv_claimed.add(k)
    return out


class WStream:
    def __init__(self, C, name, kc, gw, nbuf, n_slabs=0, cache=None):
        self.C = C
        self.name = name
        self.kc, self.gw = kc, gw
        self.rot = Rot(name, [C.sb(f"{name}{i}", [128, kc, gw], BF16) for i in range(nbuf)])
        self.cache = None
        self.pending = None
        if cache is not None:
            self.cache = cache
        elif n_slabs:
            C.ncache = getattr(C, "ncache", 0) + 1
            self.cache = C.nc.dram_tensor(f"wc_{name}_{C.ncache}", [n_slabs, 128, kc * gw], BF16).ap()

    def load(self, w_ap, c0, width=None, slab=None, first=True):
        C = self.C
        P = C.P
        width = width or self.gw
        buf, key = self.rot.next()
        if self.cache is not None and not first:
            src = self.cache[slab].rearrange("p (k f) -> p k f", k=self.kc)
            P.op("pool", lambda e: e.dma_start(out=buf[:], in_=src), reads=[("wc", self.name, slab)], writes=[key], dma=True)
            return buf, key
        wv = w_ap.rearrange("(k p) f -> p k f", p=128)
        P.op("pool", lambda e: e.dma_start(out=buf[:, :, 0:width], in_=wv[:, :, c0:c0 + width]),
             writes=[key], dma=True)
        self.flush()
        if self.cache is not None:
            dst = self.cache[slab].rearrange("p (k f) -> p k f", k=self.kc)
            self.pending = (buf, key, dst, slab)
        return buf, key

    def flush(self):
        if self.pending is not None:
            buf, key, dst, slab = self.pending
            self.pending = None
            self.C.P.op("pool", lambda e: e.dma_start(out=dst, in_=buf[:]), reads=[key], writes=[("wc", self.name, slab)], dma=True)


def postnorm_gen(C, R, t, YT, ykeyf, gname, gscale, nchunks=KC, order=None, final=False, accum=False):
    P = C.P
    bank = 7
    for c in range(nchunks):
        sq, sk = R.sq2.next()
        P.op("dve", lambda e, sq=sq, c=c: e.tensor_tensor(sq[:], YT[:, c, :], YT[:, c, :], ALU.mult),
             reads=[ykeyf(c)], writes=[sk])
        P.op("pe", lambda e, sq=sq, c=c: e.matmul(C.ps[:, bank, :], lhsT=C.ones_b[:], rhs=sq[:],
                                                   start=(c == 0), stop=(c == nchunks - 1)),
             reads=[sk, "ones_b"], writes=[("ps", bank)])
        if c % 4 == 3:
            yield
    rs, rk = R.rs2.next()
    rstd_from_sum(C, C.ps[:, bank, :], T, D, rs, ("ps", bank), rk)
    if gscale != 1.0:
        P.op("dve", lambda e, rs=rs: e.tensor_scalar(rs[:, 0:T], rs[:, 0:T], float(gscale), None, ALU.mult),
             reads=[rk], writes=[rk])
    yield
    yv = C.y_out.rearrange("(n p) d -> p n d", p=128)

    def emit_out(xr, xk, c):
        for bq in range(T // 128):
            P.op("pe", lambda e, xr=xr, bq=bq: e.transpose(C.ps[:, bank, bq * 128:(bq + 1) * 128], xr[:, bq * 128:(bq + 1) * 128],
                                                            C.ident_f[:]),
                 reads=[xk, "ident_f"], writes=[("ps", bank)])
        ob, obk = R.ob.next()
        P.op("act", lambda e, ob=ob: e.copy(ob[:], C.ps[:, bank, :].rearrange("p (a b) -> p a b", a=T // 128)),
             reads=[("ps", bank)], writes=[obk])
        P.op("sp", lambda e, ob=ob, c=c: e.dma_start(out=yv[:, t * (T // 128):(t + 1) * (T // 128), c * 128:(c + 1) * 128], in_=ob[:]),
             reads=[obk], writes=[("y", t, c)], dma=True)

    prev = None
    for c in (order if order is not None else range(nchunks)):
        if accum:
            P.op("dve", lambda e, c=c, rs=rs: e.scalar_tensor_tensor(
                out=YT[:, c, :], in0=YT[:, c, :], scalar=col(C, gname, c), in1=rs[:, 0:T], op0=ALU.mult, op1=ALU.mult),
                reads=[ykeyf(c), rk, "cols"], writes=[ykeyf(c)])
            P.op("pool", lambda e, c=c: e.dma_start(out=C.xTv[:, c, t * T:(t + 1) * T], in_=YT[:, c, :], accum_op=ALU.add),
                 reads=[ykeyf(c)], writes=[("xT", t, c)], dma=True)
            yield
            continue
        xr, xk = R.xr.next()
        P.op("sp", lambda e, xr=xr, c=c: e.dma_start(out=xr[:], in_=C.xTv[:, c, t * T:(t + 1) * T]),
             reads=[("xT", t, c)], writes=[xk], dma=True)
        P.op("dve", lambda e, c=c, rs=rs: e.scalar_tensor_tensor(
            out=YT[:, c, :], in0=YT[:, c, :], scalar=col(C, gname, c), in1=rs[:, 0:T], op0=ALU.mult, op1=ALU.mult),
            reads=[ykeyf(c), rk, "cols"], writes=[ykeyf(c)])
        P.op("dve", lambda e, c=c, xr=xr: e.tensor_tensor(xr[:], xr[:], YT[:, c, :], ALU.add),
             reads=[ykeyf(c), xk], writes=[xk])
        if not final:
            P.op("sp", lambda e, xr=xr, c=c: e.dma_start(out=C.xTv[:, c, t * T:(t + 1) * T], in_=xr[:]),
                 reads=[xk], writes=[("xT", t, c)], dma=True)
        else:
            if prev is not None:
                emit_out(*prev)
            prev = (xr, xk, c)
        yield
    if prev is not None:
        emit_out(*prev)
        yield


def postnorm_residual_tile(C, R, t, YT, ykeyf, gname, gscale, nchunks=KC, accum=False):
    for _ in postnorm_gen(C, R, t, YT, ykeyf, gname, gscale, nchunks, accum=accum):
        pass


def norm_bufs(C, R, nxt=2, nsq=2, nxr=2):
    R.xt = Rot("xt", [C.sb(f"xt{i}", [128, KC, 128]) for i in range(nxt)])
    R.sq = Rot("sq", [C.sb(f"sq{i}", [128, 4, 128], BF16) for i in range(nsq)])
    R.rs = Rot("rs", [C.sb(f"rs{i}", [128, 128]) for i in range(2)])
    R.sq2 = Rot("sq2", [C.sb(f"sq2{i}", [128, T], BF16) for i in range(3)])
    R.rs2 = Rot("rs2", [C.sb(f"rs2{i}", [128, T]) for i in range(1)])
    R.xr = Rot("xr", [C.sb(f"xr{i}", [128, T]) for i in range(nxr)])


def stage_ffn(C, tiles, ab, l, final=False):
    nc, P = C.nc, C.P
    R = Ctx()
    st_name = f"ffn_{ab}{l}"
    idx = C.stages.index(st_name)
    todo = upcoming_ffns(C, st_name, 1) if (idx + 1 < len(C.stages) and C.stages[idx + 1] == "odd") else []
    bgc = chain_gens(*[convert_gen(C, ab_, l_) for ab_, l_ in todo])
    bg_every = max(1, (22 * max(1, len(tiles))) // (60 * len(todo))) if todo else 0
    norm_bufs(C, R, nsq=4)
    if final:
        R.ob = Rot("ob", [C.sb(f"ob{i}", [128, T // 128, 128]) for i in range(2)])
        R.xr = Rot("xr", R.xr.bufs + [C.sb("xr_extra", [128, T])])
    R.pending_post = None
    hTs = [C.sb(f"hT{i}", [128, KC, T], BF16) for i in range(2)]
    AT = C.sb("AT", [128, FC, T], BF16)
    YT = C.sb("YT", [128, KC, T])
    GW = 256
    ch = C.fcache[(ab, l)]
    pre_done = (ab, l) in C.fconv_done
    wg_s = WStream(C, f"wg{ab}{l}", KC, GW, 2, cache=ch["g"])
    wu_s = WStream(C, f"wu{ab}{l}", KC, GW, 2, cache=ch["u"])
    wd_s = WStream(C, f"wd{ab}{l}", FC, 128, 2, cache=ch["d"])
    silu_t = Rot("silu", [C.sb(f"silu{i}", [128, T]) for i in range(2)])
    wg, wu, wd = (C.w[f"ffn_{ab}_w_{n}{l}"] for n in ("gate", "up", "down"))
    pre, post = f"ffn_{ab}_pre_g{l}", f"ffn_{ab}_post_g{l}"

    def tile_body(ti, t):
        hT = hTs[ti % 2]
        hkey = ("hT", ti % 2)
        if ti == 0:
            prenorm_tile(C, R, t, pre, hT, hkey)
        pend = R.pending_post
        R.pending_post = None
        if pend is not None and not PIPE_POST:
            for _ in pend:
                pass
            pend = None
        hreads = [(hkey, k, b) for k in range(KC) for b in range(T // 128)]
        n = 0
        for g in range(DFF // GW):
            gb, gk = wg_s.load(wg, g * GW, slab=g, first=(ti == 0 and not pre_done))
            ub, uk = wu_s.load(wu, g * GW, slab=g, first=(ti == 0 and not pre_done))
            for jj in range(GW // 128):
                j = g * (GW // 128) + jj
                bg, bu = (n % 2) * 2, (n % 2) * 2 + 1
                n += 1
                for k in range(KC):
                    P.op("pe", lambda e, gb=gb, k=k, jj=jj, bg=bg: e.matmul(
                        C.ps[:, bg, :], lhsT=gb[:, k, jj * 128:(jj + 1) * 128], rhs=hT[:, k, :],
                        start=(k == 0), stop=(k == KC - 1)),
                        reads=[gk] + (hreads if k in (0, KC - 1) else []), writes=[("ps", bg)])
                for k in range(KC):
                    P.op("pe", lambda e, ub=ub, k=k, jj=jj, bu=bu: e.matmul(
                        C.ps[:, bu, :], lhsT=ub[:, k, jj * 128:(jj + 1) * 128], rhs=hT[:, k, :],
                        start=(k == 0), stop=(k == KC - 1)),
                        reads=[uk] + (hreads if k == KC - 1 else []), writes=[("ps", bu)])
                st, sk = silu_t.next()
                P.op("act", lambda e, st=st, bg=bg: e.activation(out=st[:], in_=C.ps[:, bg, :], func=AF.Silu),
                     reads=[("ps", bg)], writes=[sk])
                P.op("dve", lambda e, st=st, bu=bu, j=j: e.tensor_tensor(AT[:, j, :], st[:], C.ps[:, bu, :], ALU.mult),
                     reads=[sk, ("ps", bu)], writes=[("AT", j)])
            if pend is not None:
                next(pend, None)
            if bg_every and g % bg_every == 0:
                next(bgc, None)
        if pend is not None:
            for _ in pend:
                pass
        pre_gen = None
        if ti + 1 < len(tiles):
            pre_gen = prenorm_gen(C, R, tiles[ti + 1], pre, hTs[(ti + 1) % 2], ("hT", (ti + 1) % 2))
            if PIPE_PRE:
                next(pre_gen, None)
            else:
                for _ in pre_gen:
                    pass
                pre_gen = None
        areads = [("AT", j) for j in range(FC)]
        for c in range(KC):
            db, dk = wd_s.load(wd, c * 128, slab=c, first=(ti == 0 and not pre_done))
            bank = 4 + c % 2
            for j in range(FC):
                P.op("pe", lambda e, db=db, j=j, bank=bank: e.matmul(
                    C.ps[:, bank, :], lhsT=db[:, j, :], rhs=AT[:, j, :], start=(j == 0), stop=(j == FC - 1)),
                    reads=[dk] + (areads if j in (0, FC - 1) else []), writes=[("ps", bank)])
            P.op("act", lambda e, c=c, bank=bank: e.copy(YT[:, c, :], C.ps[:, bank, :]),
                 reads=[("ps", bank)], writes=[("YT", c)])
            if pre_gen is not None and c % 3 == 1:
                next(pre_gen, None)
                next(pre_gen, None)
        if pre_gen is not None:
            for _ in pre_gen:
                pass
        for ws__ in (wg_s, wu_s, wd_s):
            ws__.flush()
        R.pending_post = postnorm_gen(C, R, t, YT, lambda c: ("YT", c), post, 0.5, final=final)
        if ti == len(tiles) - 1:
            for _ in R.pending_post:
                pass
            R.pending_post = None

    for ti_, t_ in enumerate(tiles):
        tile_body(ti_, t_)
    for _ in bgc:
        pass


def stage_odd(C, tiles):
    nc, P = C.nc, C.P
    R = Ctx()
    norm_bufs(C, R, nsq=4, nxr=6)
    R.pending_post = None
    NB = T // 128
    hTs = [C.sb(f"o_hT{i}", [128, KC, T], BF16) for i in range(2)]
    big = C.sb("o_big", [128, KC, T])
    YT = big

    def vgv(b):
        return big[:, b * 4:(b + 1) * 4, :]

    def bk(slot):
        return ("o_big", slot)

    vln = C.sb("o_vln", [128, NB, D], BF16)
    gT = C.sb("o_gT", [128, KC, T], BF16)
    vgb = C.sb("o_vgb", [128, D])
    vbb = C.sb("o_vbb", [128, D])
    bsb = C.sb("o_bsb", [128, 8 * 128])
    wsT = C.sb("o_wsT", [128, 8, 128], BF16)
    wsl = C.sb("o_wsl", [128, 8, 128])
    stat = C.sb("o_stat", [128, 8])
    tmps = Rot("o_tmps", [C.sb(f"o_tmps{i}", [128, T]) for i in range(2)])
    ugel = Rot("o_ugel", [C.sb(f"o_ugel{i}", [128, T]) for i in range(2)])
    ws_ = WStream(C, "o_w", KC, 256, 4, n_slabs=24)
    w_in, w_out = C.w["odd_w_in"], C.w["odd_w_out"]
    todo = []
    bg = chain_gens(*[convert_gen(C, ab_, l_) for ab_, l_ in todo])
    bg_steps = -(-60 * len(todo) // (8 * max(1, len(tiles))))

    P.op("sp", lambda e: e.dma_start(out=vgb[:], in_=C.odd_vg.partition_broadcast(128)), writes=["vgb"], dma=True)
    P.op("sp", lambda e: e.dma_start(out=vbb[:], in_=C.odd_vb.partition_broadcast(128)), writes=["vbb"], dma=True)
    P.op("sp", lambda e: e.dma_start(out=bsb[:], in_=C.odd_bs.partition_broadcast(128)), writes=["bsb"], dma=True)
    P.op("sp", lambda e: e.dma_start(out=wsl[:], in_=C.odd_ws.rearrange("g t s -> t g s")), writes=["wsl"], dma=True)
    for g in range(8):
        bank = g % 2
        P.op("pe", lambda e, g=g, bank=bank: e.transpose(C.ps[:, bank, 0:128], wsl[:, g, :], C.ident_f[:]),
             reads=["wsl", "ident_f"], writes=[("ps", bank)])
        P.op("dve", lambda e, g=g, bank=bank: e.tensor_copy(wsl[:, g, :], C.ps[:, bank, 0:128]),
             reads=[("ps", bank)], writes=["wsl"])
        P.op("pool", lambda e, g=g: e.affine_select(out=wsl[:, g, :], in_=wsl[:, g, :], pattern=[[1, 128]],
                                                    compare_op=ALU.is_ge, fill=0.0, base=0, channel_multiplier=-1),
             reads=["wsl"], writes=["wsl"])
        P.op("dve", lambda e, g=g: e.tensor_copy(wsT[:, g, :], wsl[:, g, :]), reads=["wsl"], writes=[("wsT", g)])

    def tile_body(ti, t):
        hT = hTs[ti % 2]
        hkey = ("o_hT", ti % 2)
        if ti == 0:
            prenorm_tile(C, R, t, "odd_pre_g", hT, hkey)
        hreads = [(hkey, k, b) for k in range(KC) for b in range(NB)]
        pend = R.pending_post
        R.pending_post = None
        if pend is not None:
            for _ in range(4):
                next(pend, None)
        n = 0
        for nb_ in range(8):
            vb_, vk = ws_.load(w_in, D + nb_ * 256, slab=nb_, first=(ti == 0))
            for b in range(NB):
                bank = 2 + n % 2
                n += 1
                if pend is not None and nb_ % 2 == 0:
                    next(pend, None)
                for k in range(KC):
                    P.op("pe", lambda e, vb_=vb_, k=k, b=b, bank=bank: e.matmul(
                        C.ps[:, bank, 0:256], lhsT=hT[:, k, b * 128:(b + 1) * 128], rhs=vb_[:, k, :],
                        start=(k == 0), stop=(k == KC - 1)),
                        reads=[vk] + (hreads if k in (0, KC - 1) else []), writes=[("ps", bank)])
                slot = b * 4 + nb_ // 2
                P.op("act", lambda e, b=b, nb_=nb_, bank=bank, slot=slot: e.activation(
                    out=big[:, slot, (nb_ % 2) * 256:(nb_ % 2 + 1) * 256], in_=C.ps[:, bank, 0:256], func=AF.Gelu),
                    reads=[("ps", bank)], writes=[bk(slot)])
        if pend is not None:
            for _ in pend:
                pass
        for b in range(NB):
            vr = [bk(b * 4 + q) for q in range(4)]
            P.op("dve", lambda e, b=b: e.tensor_reduce(out=stat[:, 0:1], in_=vgv(b), axis=AX.XY, op=ALU.add),
                 reads=vr, writes=["o_stat0"])
            P.op("dve", lambda e: e.tensor_scalar(stat[:, 1:2], stat[:, 0:1], -1.0 / D, None, ALU.mult),
                 reads=["o_stat0"], writes=["o_stat1"])
            P.op("dve", lambda e, b=b: e.tensor_scalar(vgv(b), vgv(b), stat[:, 1:2], None, ALU.add),
                 reads=vr + ["o_stat1"], writes=vr)
            P.op("act", lambda e, b=b: e.activation(out=vln[:, b, :].rearrange("p (a c) -> p a c", a=4), in_=vgv(b),
                                                    func=AF.Square, accum_out=stat[:, 2:3]),
                 reads=vr, writes=[("o_vln", b), "o_stat2"])
            P.op("dve", lambda e: e.tensor_scalar(stat[:, 3:4], stat[:, 2:3], 1.0 / D, EPS, ALU.mult, ALU.add),
                 reads=["o_stat2"], writes=["o_stat3"])
            P.op("act", lambda e: e.activation(out=stat[:, 3:4], in_=stat[:, 3:4], func=AF.Sqrt),
                 reads=["o_stat3"], writes=["o_stat3"])
            P.op("dve", lambda e: e.reciprocal(stat[:, 4:5], stat[:, 3:4]), reads=["o_stat3"], writes=["o_stat4"])
            P.op("dve", lambda e, b=b: e.scalar_tensor_tensor(out=vgv(b), in0=vgv(b), scalar=stat[:, 4:5],
                                                              in1=vgb[:].rearrange("p (a c) -> p a c", a=4),
                                                              op0=ALU.mult, op1=ALU.mult),
                 reads=vr + ["o_stat4", "vgb"], writes=vr)
            P.op("dve", lambda e, b=b: e.tensor_tensor(vln[:, b, :].rearrange("p (a c) -> p a c", a=4), vgv(b),
                                                       vbb[:].rearrange("p (a c) -> p a c", a=4), ALU.add),
                 reads=vr + ["vbb"], writes=[("o_vln", b)])
        n = 0
        for g2 in range(D // 256):
            ub, uk = ws_.load(w_in, g2 * 256, slab=8 + g2, first=(ti == 0))
            for jj in range(2):
                j = g2 * 2 + jj
                g = j // 2
                bs_ = 4 + n % 2
                bu_ = n % 2
                n += 1
                for b in range(NB):
                    P.op("pe", lambda e, j=j, b=b, g=g, bs_=bs_: e.matmul(
                        C.ps[:, bs_, b * 128:(b + 1) * 128], lhsT=vln[:, b, j * 128:(j + 1) * 128], rhs=wsT[:, g, :],
                        start=True, stop=True),
                        reads=[("o_vln", b), ("wsT", g)], writes=[("ps", bs_)])
                for k in range(KC):
                    P.op("pe", lambda e, ub=ub, k=k, jj=jj, bu_=bu_: e.matmul(
                        C.ps[:, bu_, :], lhsT=ub[:, k, jj * 128:(jj + 1) * 128], rhs=hT[:, k, :],
                        start=(k == 0), stop=(k == KC - 1)),
                        reads=[uk] + (hreads if k in (0, KC - 1) else []), writes=[("ps", bu_)])
                ug, ugk = ugel.next()
                P.op("act", lambda e, ug=ug, bu_=bu_: e.activation(out=ug[:], in_=C.ps[:, bu_, :], func=AF.Gelu),
                     reads=[("ps", bu_)], writes=[ugk])
                ts_, tk = tmps.next()
                for b in range(NB):
                    P.op("dve", lambda e, ts_=ts_, b=b, g=g, bs_=bs_: e.tensor_tensor(
                        ts_[:, b * 128:(b + 1) * 128], C.ps[:, bs_, b * 128:(b + 1) * 128], bsb[:, g * 128:(g + 1) * 128], ALU.add),
                        reads=[("ps", bs_), "bsb"], writes=[tk])
                P.op("dve", lambda e, ts_=ts_, ug=ug, j=j: e.tensor_tensor(gT[:, j, :], ts_[:], ug[:], ALU.mult),
                     reads=[tk, ugk], writes=[("o_gT", j)])
            for _ in range(bg_steps):
                next(bg, None)
        greads = [("o_gT", j) for j in range(KC)]
        pre_gen = None
        if ti + 1 < len(tiles):
            pre_gen = prenorm_gen(C, R, tiles[ti + 1], "odd_pre_g", hTs[(ti + 1) % 2], ("o_hT", (ti + 1) % 2))
            next(pre_gen, None)
        n = 0
        for g2 in range(D // 256):
            if pre_gen is not None and g2 >= 1:
                next(pre_gen, None)
            ob, ok = ws_.load(w_out, g2 * 256, slab=16 + g2, first=(ti == 0))
            for jj in range(2):
                c = g2 * 2 + jj
                bank = 2 + n % 2
                n += 1
                for k in range(KC):
                    P.op("pe", lambda e, ob=ob, k=k, jj=jj, bank=bank: e.matmul(
                        C.ps[:, bank, :], lhsT=ob[:, k, jj * 128:(jj + 1) * 128], rhs=gT[:, k, :],
                        start=(k == 0), stop=(k == KC - 1)),
                        reads=[ok] + (greads if k in (0, KC - 1) else []), writes=[("ps", bank)])
                P.op("act", lambda e, c=c, bank=bank: e.copy(YT[:, c, :], C.ps[:, bank, :]),
                     reads=[("ps", bank)], writes=[bk(c)])
        if pre_gen is not None:
            for _ in pre_gen:
                pass
        ws_.flush()
        order = [b * 4 + q for q in range(4) for b in range(NB)]
        gen = postnorm_gen(C, R, t, YT, bk, "odd_post_g", 1.0, order=order)
        for _ in range(5):
            next(gen, None)
        if ti == len(tiles) - 1:
            for _ in gen:
                pass
        else:
            R.pending_post = gen

    for ti_, t_ in enumerate(tiles):
        tile_body(ti_, t_)
    for _ in bg:
        pass


def stage_even(C, tiles):
    nc, P = C.nc, C.P
    R = Ctx()
    norm_bufs(C, R, nxt=1)
    NB = T // 128
    SCALE = float((128 + 64) ** -0.5)
    A = C.sb("e_A", [128, KC * T])
    Ab = A.bitcast(BF16)
    YT = A[:].rearrange("p (k t) -> p k t", k=KC)
    hT = Ab[:, 0:KC * T].rearrange("p (k t) -> p k t", k=KC)
    qn = Ab[:, KC * T:KC * T + 8 * T].rearrange("p (k t) -> p k t", k=8)
    qp = Ab[0:64, KC * T + 8 * T:KC * T + 16 * T].rearrange("p (k t) -> p k t", k=8)
    B = C.sb("e_B", [128, KC * T], BF16)
    wuq = B[:, 0:4 * 1536].rearrange("p (k f) -> p k f", k=4)
    wukv = B[:, 0:4 * 2048].rearrange("p (k f) -> p k f", k=4)
    aT = B[:].rearrange("p (k t) -> p k t", k=KC)
    Bkeys = [("e_B", i) for i in range(KC)]
    clat = C.sb("e_clat", [128, 4, T])
    clatn = C.sb("e_clatn", [128, 4, T], BF16)
    wuq_sw = C.sb("e_wuqsw", [128, 4, 8, 64], BF16)
    wkr = C.sb("e_wkr", [128, KC, 128], BF16)
    knT = C.sb("e_knT", [128, 8, SEQ], BF16)
    kpT = C.sb("e_kpT", [64, SEQ], BF16)
    vtok = C.sb("e_vtok", [128, SEQ // 128, 1024], BF16)
    halo = C.sb("e_halo", [128, 8, 32], BF16)
    zc = Rot("e_zc", [C.sb(f"e_zc{i}", [128, 32 + T], BF16) for i in range(2)])
    dg = C.sb("e_dg", [128, 31, 128], BF16)
    acc = Rot("e_acc", [C.sb(f"e_acc{i}", [128, T]) for i in range(2)])
    sg = Rot("e_sg", [C.sb(f"e_sg{i}", [128, T]) for i in range(1)])
    tmp = Rot("e_tmp", [C.sb(f"e_tmp{i}", [128, T]) for i in range(2)])
    pT = Rot("e_pT", [C.sb(f"e_pT{i}", [128, T], BF16) for i in range(2)])
    cs = C.sb("e_cos", [64, T])
    angb = C.sb("e_ang", [128, T])
    sn = C.sb("e_sin", [64, T])
    mask = C.sb("e_mask", [128, 4, T], BF16)
    ws_ = WStream(C, "e_w", KC, 256, 2, n_slabs=28)
    w_in, w_out = C.w["even_w_in"], C.w["even_w_out"]
    w_uq, w_ukv = C.w["even_w_uq"], C.w["even_w_ukv"]
    OFF_CONV = 1088
    todo = upcoming_ffns(C, "even", 2)
    bg = chain_gens(*[convert_gen(C, ab_, l_) for ab_, l_ in todo])
    bg_steps = -(-60 * len(todo) // (8 * max(1, len(tiles))))
    icol = COLS["inv_freq"][0]
    scol = COLS["sin_sign"][0]

    uqv = w_uq.rearrange("(k p) (h c) -> p k h c", p=128, c=192)
    for k in range(4):
        P.op("pool", lambda e, k=k: e.dma_start(out=wuq_sw[:, k, :, 0:32], in_=uqv[:, k, :, 160:192]), writes=[("wuqsw0", k)], dma=True)
        P.op("pool", lambda e, k=k: e.dma_start(out=wuq_sw[:, k, :, 32:64], in_=uqv[:, k, :, 128:160]), writes=[("wuqsw1", k)], dma=True)
    winv = w_in.rearrange("(k p) f -> p k f", p=128)
    P.op("pool", lambda e: e.dma_start(out=wkr[:, :, 0:64], in_=winv[:, :, 1024:1088]), writes=["wkr0"], dma=True)
    P.op("pool", lambda e: e.dma_start(out=wkr[:, :, 64:96], in_=winv[:, :, 1056:1088]), writes=["wkr1"], dma=True)
    P.op("pool", lambda e: e.dma_start(out=wkr[:, :, 96:128], in_=winv[:, :, 1024:1056]), writes=["wkr2"], dma=True)
    mf, mfk = tmp.next()
    for d in range(4):
        P.op("pool", lambda e: e.memset(mf[:], -30000.0), writes=[mfk])
        P.op("pool", lambda e, d=d: e.affine_select(out=mf[:], in_=mf[:], pattern=[[-1, T]], compare_op=ALU.is_ge,
                                                    fill=0.0, base=d * 128 - 1, channel_multiplier=1),
             reads=[mfk], writes=[mfk])
        P.op("pool", lambda e, d=d: e.tensor_copy(mask[:, d, :], mf[:]), reads=[mfk], writes=[("mask", d)])
    P.barrier()
    setup_keys = {}

    def tile_body(ti, t):
        if ti > 0:
            P.barrier()
        ts = t % (SEQ // T)
        s0 = ts * T
        if ts == 0:
            P.op("dve", lambda e: e.memset(halo[:], 0.0), writes=[("halo", c) for c in range(8)])
        prenorm_tile(C, R, t, "even_pre_g", hT, "e_hT")
        hreads = [("e_hT", k, b) for k in range(KC) for b in range(NB)]
        posi, pk_ = tmp.next()
        ang, ak_ = angb, "e_ang"
        posi_i = posi.bitcast(I32)
        P.op("sp", lambda e, t=t, posi_i=posi_i: e.dma_start(out=posi_i[0:64, :], in_=C.pos[:, t * T:(t + 1) * T].partition_broadcast(64)),
             writes=[pk_], dma=True)
        P.op("dve", lambda e, posi_i=posi_i, ang=ang: e.tensor_copy(ang[0:64, :], posi_i[0:64, :]), reads=[pk_], writes=[ak_])
        P.op("dve", lambda e, ang=ang: e.tensor_scalar(ang[0:64, :], ang[0:64, :], C.cols[0:64, icol:icol + 1], None, ALU.mult),
             reads=[ak_, "cols"], writes=[ak_])
        C1 = 6.28125
        C2 = float(2 * np.pi - 6.28125)
        INV2PI = float(1.0 / (2 * np.pi))
        for dst, dkey, off in ((sn, "sn", 0.0), (cs, "cs", float(np.pi / 2))):
            kb_, kk_ = tmp.next()
            kbi = kb_.bitcast(I32)
            P.op("dve", lambda e, kbi=kbi, ang=ang, off=off: e.tensor_scalar(kbi[0:64, :], ang[0:64, :], INV2PI, off * INV2PI, ALU.mult, ALU.add),
                 reads=[ak_], writes=[kk_])
            P.op("dve", lambda e, kbi=kbi, dst=dst: e.tensor_copy(dst[:], kbi[0:64, :]), reads=[kk_], writes=[dkey])
            P.op("dve", lambda e, kb_=kb_, dst=dst, ang=ang: e.scalar_tensor_tensor(out=kb_[0:64, :], in0=dst[:], scalar=-C1, in1=ang[0:64, :],
                                                                                    op0=ALU.mult, op1=ALU.add),
                 reads=[dkey, ak_, kk_], writes=[kk_])
            P.op("dve", lambda e, kb_=kb_, dst=dst: e.scalar_tensor_tensor(out=kb_[0:64, :], in0=dst[:], scalar=-C2, in1=kb_[0:64, :],
                                                                           op0=ALU.mult, op1=ALU.add),
                 reads=[dkey, kk_], writes=[kk_])
            P.op("dve", lambda e, kb_=kb_, dst=dst, off=off: e.tensor_scalar(dst[:], kb_[0:64, :], -1.0, float(np.pi) - off, ALU.mult, ALU.add),
                 reads=[kk_, dkey], writes=[dkey])
            P.op("dve", lambda e, kb_=kb_, off=off: e.tensor_scalar(kb_[0:64, :], kb_[0:64, :], off, None, ALU.add),
                 reads=[kk_], writes=[kk_])
            P.op("dve", lambda e, kb_=kb_, dst=dst: e.tensor_tensor(dst[:], dst[:], kb_[0:64, :], ALU.min),
                 reads=[kk_, dkey], writes=[dkey])
            P.op("dve", lambda e, dst=dst: e.tensor_scalar(dst[:], dst[:], float(-np.pi), float(np.pi), ALU.max, ALU.min),
                 reads=[dkey], writes=[dkey])
            P.op("act", lambda e, dst=dst: e.activation(out=dst[:], in_=dst[:], func=AF.Sin), reads=[dkey], writes=[dkey])
        P.op("dve", lambda e: e.tensor_scalar(sn[:], sn[:], C.cols[0:64, scol:scol + 1], None, ALU.mult),
             reads=["sn", "cols"], writes=["sn"])

        def rope_evac(ps_x, ps_sw, out_ap, okeys, banks):
            t1, k1 = tmp.next()
            t2, k2 = tmp.next()
            P.op("dve", lambda e: e.tensor_tensor(t1[0:64, :], ps_x, cs[:], ALU.mult),
                 reads=[("ps", banks[0]), "cs"], writes=[k1])
            P.op("dve", lambda e: e.tensor_tensor(t2[0:64, :], ps_sw, sn[:], ALU.mult),
                 reads=[("ps", banks[1]), "sn"], writes=[k2])
            P.op("dve", lambda e: e.tensor_tensor(out_ap, t1[0:64, :], t2[0:64, :], ALU.add),
                 reads=[k1, k2], writes=okeys)

        def latent(col0, gname):
            n = 0
            for g in range(2):
                wb, wk = ws_.load(w_in, col0 + g * 256, slab=col0 // 256 + g, first=(ti == 0))
                for jj in range(2):
                    j = g * 2 + jj
                    bank = n % 2
                    n += 1
                    for k in range(KC):
                        P.op("pe", lambda e, wb=wb, k=k, jj=jj, bank=bank: e.matmul(
                            C.ps[:, bank, :], lhsT=wb[:, k, jj * 128:(jj + 1) * 128], rhs=hT[:, k, :],
                            start=(k == 0), stop=(k == KC - 1)),
                            reads=[wk] + (hreads if k in (0, KC - 1) else []), writes=[("ps", bank)])
                    P.op("act", lambda e, j=j, bank=bank: e.copy(clat[:, j, :], C.ps[:, bank, :]),
                         reads=[("ps", bank)], writes=[("clat", j)])
            bank = 6
            for c in range(4):
                sq, sk = R.sq2.next()
                P.op("dve", lambda e, sq=sq, c=c: e.tensor_tensor(sq[:], clat[:, c, :], clat[:, c, :], ALU.mult),
                     reads=[("clat", c)], writes=[sk])
                P.op("pe", lambda e, sq=sq, c=c, bank=bank: e.matmul(C.ps[:, bank, :], lhsT=C.ones_b[:], rhs=sq[:],
                                                                      start=(c == 0), stop=(c == 3)),
                     reads=[sk, "ones_b"], writes=[("ps", bank)])
            rs, rk = R.rs2.next()
            rstd_from_sum(C, C.ps[:, bank, :], T, 512, rs, ("ps", bank), rk)
            for c in range(4):
                P.op("dve", lambda e, c=c, rs=rs: e.scalar_tensor_tensor(
                    out=clatn[:, c, :], in0=clat[:, c, :], scalar=col(C, gname, c), in1=rs[:, 0:T], op0=ALU.mult, op1=ALU.mult),
                    reads=[("clat", c), rk, "cols"], writes=[("clatn", c)])

        lreads = [("clatn", c) for c in range(4)]
        P.op("pool", lambda e: e.dma_start(out=wuq, in_=w_uq.rearrange("(k p) f -> p k f", p=128)), writes=Bkeys[0:12], dma=True)
        latent(0, "q_norm_g")
        for h in range(8):
            bank = 2 + h % 2
            for k in range(4):
                P.op("pe", lambda e, h=h, k=k, bank=bank: e.matmul(
                    C.ps[:, bank, :], lhsT=wuq[:, k, h * 192:h * 192 + 128], rhs=clatn[:, k, :],
                    start=(k == 0), stop=(k == 3)),
                    reads=Bkeys[0:12] + lreads, writes=[("ps", bank)])
            P.op("act", lambda e, h=h, bank=bank: e.copy(qn[:, h, :], C.ps[:, bank, :]),
                 reads=[("ps", bank)], writes=[("qn", h)])
            b0, b1 = (4, 5) if h % 2 == 0 else (0, 1)
            for k in range(4):
                P.op("pe", lambda e, h=h, k=k, b0=b0: e.matmul(
                    C.ps[0:64, b0, :], lhsT=wuq[:, k, h * 192 + 128:h * 192 + 192], rhs=clatn[:, k, :],
                    start=(k == 0), stop=(k == 3)),
                    reads=Bkeys[0:12] + lreads, writes=[("ps", b0)])
            for k in range(4):
                P.op("pe", lambda e, h=h, k=k, b1=b1: e.matmul(
                    C.ps[0:64, b1, :], lhsT=wuq_sw[:, k, h, :], rhs=clatn[:, k, :],
                    start=(k == 0), stop=(k == 3)),
                    reads=lreads, writes=[("ps", b1)])
            rope_evac(C.ps[0:64, b0, :], C.ps[0:64, b1, :], qp[:, h, :], [("qp", h)], (b0, b1))
        P.op("pool", lambda e: e.dma_start(out=wukv, in_=w_ukv.rearrange("(k p) f -> p k f", p=128)), writes=Bkeys, dma=True)
        latent(512, "kv_norm_g")
        for h in range(8):
            bank = 2 + h % 2
            for k in range(4):
                P.op("pe", lambda e, h=h, k=k, bank=bank: e.matmul(
                    C.ps[:, bank, :], lhsT=wukv[:, k, h * 256:h * 256 + 128], rhs=clatn[:, k, :],
                    start=(k == 0), stop=(k == 3)),
                    reads=Bkeys + lreads, writes=[("ps", bank)])
            P.op("act", lambda e, h=h, bank=bank: e.copy(knT[:, h, s0:s0 + T], C.ps[:, bank, :]),
                 reads=[("ps", bank)], writes=[("knT", h)])
        wukv_v = wukv.rearrange("p k (h c) -> p k h c", c=256)
        for b in range(NB):
            for half in range(2):
                bank = 4 + (b * 2 + half) % 2
                for k in range(4):
                    P.op("pe", lambda e, b=b, k=k, half=half, bank=bank: e.matmul(
                        C.ps[:, bank, :].rearrange("p (h c) -> p h c", c=128),
                        lhsT=clatn[:, k, b * 128:(b + 1) * 128], rhs=wukv_v[:, k, half * 4:(half + 1) * 4, 128:256],
                        start=(k == 0), stop=(k == 3)),
                        reads=Bkeys + lreads, writes=[("ps", bank)])
                P.op("dve", lambda e, b=b, half=half, bank=bank: e.tensor_copy(
                    vtok[:, ts * NB + b, half * 512:(half + 1) * 512], C.ps[:, bank, :]),
                    reads=[("ps", bank)], writes=["vtok"])
        for k in range(KC):
            P.op("pe", lambda e, k=k: e.matmul(C.ps[0:64, 0, :], lhsT=wkr[:, k, 0:64], rhs=hT[:, k, :],
                                               start=(k == 0), stop=(k == KC - 1)),
                 reads=(hreads if k in (0, KC - 1) else []), writes=[("ps", 0)])
        for k in range(KC):
            P.op("pe", lambda e, k=k: e.matmul(C.ps[0:64, 1, :], lhsT=wkr[:, k, 64:128], rhs=hT[:, k, :],
                                               start=(k == 0), stop=(k == KC - 1)),
                 reads=(hreads if k in (0, KC - 1) else []), writes=[("ps", 1)])
        rope_evac(C.ps[0:64, 0, :], C.ps[0:64, 1, :], kpT[:, s0:s0 + T], ["kpT"], (0, 1))
        off_w = COLS["conv_w"][0]
        nkb = (ts + 1) * NB

        zst = {}

        def conv_proj(c):
            gb_, gk_ = ws_.load(w_in, OFF_CONV + 1024 + c * 128, 128, slab=4 + 2 * c, first=(ti == 0))
            for k in range(KC):
                P.op("pe", lambda e, gb_=gb_, k=k: e.matmul(C.ps[:, 2, :], lhsT=gb_[:, k, 0:128], rhs=hT[:, k, :],
                                                            start=(k == 0), stop=(k == KC - 1)),
                     reads=[gk_] + (hreads if k in (0, KC - 1) else []), writes=[("ps", 2)])
            sgt, sgk = sg.next()
            P.op("act", lambda e, sgt=sgt: e.activation(out=sgt[:], in_=C.ps[:, 2, :], func=AF.Sigmoid),
                 reads=[("ps", 2)], writes=[sgk])
            ab_, ak2_ = ws_.load(w_in, OFF_CONV + c * 128, 128, slab=5 + 2 * c, first=(ti == 0))
            for k in range(KC):
                P.op("pe", lambda e, ab_=ab_, k=k: e.matmul(C.ps[:, 3, :], lhsT=ab_[:, k, 0:128], rhs=hT[:, k, :],
                                                            start=(k == 0), stop=(k == KC - 1)),
                     reads=[ak2_] + (hreads if k in (0, KC - 1) else []), writes=[("ps", 3)])
            z, zk = zc.next()
            zst[c] = (z, zk)
            P.op("dve", lambda e, z=z, c=c: e.tensor_copy(z[:, 2:32], halo[:, c, 2:32]), reads=[("halo", c)], writes=[zk])
            P.op("dve", lambda e, z=z, sgt=sgt: e.tensor_tensor(z[:, 32:32 + T], C.ps[:, 3, :], sgt[:], ALU.mult),
                 reads=[("ps", 3), sgk, zk], writes=[zk])
            P.op("dve", lambda e, c=c, z=z: e.tensor_copy(halo[:, c, 2:32], z[:, T + 2:T + 32]), reads=[zk], writes=[("halo", c)])
            P.op("dve", lambda e, c=c: e.tensor_tensor(
                dg[:], bass.AP(C.ident_f, 0, [[128, 128], [0, 31], [1, 128]]),
                bass.AP(C.cols, off_w + c * 31, [[NCOL, 128], [1, 31], [0, 128]]), ALU.mult),
                reads=["ident_f", "cols"], writes=["dg"])

        def conv_mm(c):
            z, zk = zst[c]
            for j in range(31):
                P.op("pe", lambda e, z=z, j=j: e.matmul(C.ps[:, 2, :], lhsT=dg[:, j, :], rhs=z[:, 2 + j:2 + j + T],
                                                        start=(j == 0), stop=(j == 30)),
                     reads=["dg", zk], writes=[("ps", 2)])
            at, akey = acc.next()
            zst[c] = (at, akey)
            P.op("act", lambda e, at=at, c=c: e.activation(out=at[:], in_=C.ps[:, 2, :], func=AF.Identity, bias=col(C, "conv_b", c)),
                 reads=[("ps", 2), "cols"], writes=[akey])

        def conv_ln(c):
            at, akey = zst[c]
            atb, atbk = R.sq2.next()
            P.op("act", lambda e, at=at, atb=atb: e.copy(atb[:], at[:]), reads=[akey], writes=[atbk])
            P.op("pe", lambda e, atb=atb: e.matmul(C.ps[:, 3, :], lhsT=C.ones_b[:], rhs=atb[:], start=True, stop=True),
                 reads=[atbk, "ones_b"], writes=[("ps", 3)])
            P.op("dve", lambda e, at=at: e.scalar_tensor_tensor(out=at[:], in0=C.ps[:, 3, :], scalar=-1.0 / 128, in1=at[:],
                                                                 op0=ALU.mult, op1=ALU.add),
                 reads=[("ps", 3), akey], writes=[akey])
            sq, sk = R.sq2.next()
            P.op("dve", lambda e, sq=sq, at=at: e.tensor_tensor(sq[:], at[:], at[:], ALU.mult), reads=[akey], writes=[sk])
            P.op("pe", lambda e, sq=sq: e.matmul(C.ps[:, 3, :], lhsT=C.ones_b[:], rhs=sq[:], start=True, stop=True),
                 reads=[sk, "ones_b"], writes=[("ps", 3)])
            rs, rk = R.rs2.next()
            rstd_from_sum(C, C.ps[:, 3, :], T, 128, rs, ("ps", 3), rk)
            P.op("dve", lambda e, at=at, rs=rs: e.tensor_tensor(at[:], at[:], rs[:, 0:T], ALU.mult), reads=[akey, rk], writes=[akey])
            P.op("act", lambda e, at=at, c=c: e.activation(out=aT[:, 8 + c, :], in_=at[:], func=AF.Silu,
                                                           scale=col(C, "conv_ng", c), bias=col(C, "conv_nb", c)),
                 reads=[akey, "cols"], writes=[Bkeys[8 + c]])

        def attn_head(h):
            bo, br = 4 + (h % 2) * 2, 5 + (h % 2) * 2

            def s_mm(kb):
                bs_ = kb % 2
                dgn = kb - ts * NB
                P.op("pe", lambda e, h=h, kb=kb, bs_=bs_: e.matmul(
                    C.ps[:, bs_, :], lhsT=knT[:, h, kb * 128:(kb + 1) * 128], rhs=qn[:, h, :], start=True, stop=False),
                    reads=[("knT", h), ("qn", h)], writes=[("ps", bs_)])
                P.op("pe", lambda e, h=h, kb=kb, bs_=bs_, dgn=dgn: e.matmul(
                    C.ps[:, bs_, :], lhsT=kpT[:, kb * 128:(kb + 1) * 128], rhs=qp[:, h, :], start=False, stop=(dgn < 0)),
                    reads=["kpT", ("qp", h)], writes=[("ps", bs_)])
                if dgn >= 0:
                    P.op("pe", lambda e, bs_=bs_, dgn=dgn: e.matmul(
                        C.ps[:, bs_, :], lhsT=C.ident_b[:], rhs=mask[:, dgn, :], start=False, stop=True),
                        reads=["ident_b", ("mask", dgn)], writes=[("ps", bs_)])

            s_mm(0)
            for kb in range(nkb):
                bs_ = kb % 2
                pt, pk = pT.next()
                P.op("act", lambda e, pt=pt, bs_=bs_: e.activation(out=pt[:], in_=C.ps[:, bs_, :], func=AF.Exp, scale=SCALE),
                     reads=[("ps", bs_)], writes=[pk])
                if kb + 1 < nkb:
                    s_mm(kb + 1)
                P.op("pe", lambda e, h=h, kb=kb, pt=pt, bo=bo: e.matmul(
                    C.ps[:, bo, :], lhsT=vtok[:, kb, h * 128:(h + 1) * 128], rhs=pt[:], start=(kb == 0), stop=(kb == nkb - 1)),
                    reads=["vtok", pk], writes=[("ps", bo)])
                P.op("pe", lambda e, pt=pt, br=br, kb=kb: e.matmul(
                    C.ps[:, br, :], lhsT=C.ones_b[:], rhs=pt[:], start=(kb == 0), stop=(kb == nkb - 1)),
                    reads=["ones_b", pk], writes=[("ps", br)])
            rt, rk = tmp.next()
            P.op("dve", lambda e, rt=rt, br=br: e.reciprocal(rt[:], C.ps[:, br, :]), reads=[("ps", br)], writes=[rk])
            P.op("dve", lambda e, rt=rt, bo=bo, h=h: e.tensor_tensor(aT[:, h, :], C.ps[:, bo, :], rt[:], ALU.mult),
                 reads=[("ps", bo), rk], writes=[Bkeys[h]])

        conv_proj(0)
        for c in range(8):
            conv_mm(c)
            if c + 1 < 8:
                conv_proj(c + 1)
            attn_head(c)
            conv_ln(c)
            for _ in range(bg_steps):
                next(bg, None)
        n = 0
        for g in range(D // 256):
            ob, ok = ws_.load(w_out, g * 256, slab=20 + g, first=(ti == 0))
            for jj in range(2):
                c = g * 2 + jj
                bank = 2 + n % 2
                n += 1
                for k in range(KC):
                    P.op("pe", lambda e, ob=ob, k=k, jj=jj, bank=bank: e.matmul(
                        C.ps[:, bank, :], lhsT=ob[:, k, jj * 128:(jj + 1) * 128], rhs=aT[:, k, :],
                        start=(k == 0), stop=(k == KC - 1)),
                        reads=[ok] + (Bkeys if k in (0, KC - 1) else []), writes=[("ps", bank)])
                P.op("act", lambda e, c=c, bank=bank: e.copy(YT[:, c, :], C.ps[:, bank, :]),
                     reads=[("ps", bank)], writes=[("e_YT", c)])
        ws_.flush()
        postnorm_residual_tile(C, R, t, YT, lambda c: ("e_YT", c), "even_post_g", 1.0, accum=ACCUM_EVEN)

    for ti_, t_ in enumerate(tiles):
        tile_body(ti_, t_)
    for _ in bg:
        pass


ALL_STAGES = ["tin", "ffn_a0", "even", "ffn_b0", "ffn_a1", "odd", "ffn_b1"]


def colsify(v):
    v = np.asarray(v, np.float32).reshape(-1, 128)
    return np.ascontiguousarray(v.T)


def make_cols(inp):
    cols = np.zeros((128, NCOL), np.float32)

    def put(name, arr):
        off, w = COLS[name]
        assert arr.shape == (128, w), (name, arr.shape, w)
        cols[:, off:off + w] = arr

    for l in range(2):
        for ab in "ab":
            put(f"ffn_{ab}_pre_g{l}", colsify(inp[f"ffn_{ab}_pre_g"][l]))
            put(f"ffn_{ab}_post_g{l}", colsify(inp[f"ffn_{ab}_post_g"][l]))
    put("even_pre_g", colsify(inp["even_pre_g"][0]))
    put("even_post_g", colsify(inp["even_post_g"][0]))
    put("odd_pre_g", colsify(inp["odd_pre_g"][0]))
    put("odd_post_g", colsify(inp["odd_post_g"][0]))
    put("q_norm_g", colsify(inp["even_q_norm_g"][0]))
    put("kv_norm_g", colsify(inp["even_kv_norm_g"][0]))
    cw = np.asarray(inp["even_conv_w"][0], np.float32)
    put("conv_w", np.ascontiguousarray(cw.reshape(31, 8, 128).transpose(2, 1, 0).reshape(128, 248)))
    put("conv_b", colsify(inp["even_conv_b"][0]))
    put("conv_ng", colsify(inp["even_conv_norm_g"][0]))
    put("conv_nb", colsify(inp["even_conv_norm_b"][0]))
    inv_freq = (np.float32(10000.0) ** (-np.arange(0, 64, 2, dtype=np.float32) / np.float32(64))).astype(np.float32)
    c = np.zeros((128, 1), np.float32)
    c[0:64, 0] = np.concatenate([inv_freq, inv_freq])
    put("inv_freq", c)
    s = np.zeros((128, 1), np.float32)
    s[0:32, 0] = -1.0
    s[32:64, 0] = 1.0
    put("sin_sign", s)
    return cols


def make_in_maps(inp, n_cores=N_CORES):
    shared = {"cols": make_cols(inp)}
    for l in range(2):
        for ab in "ab":
            for n in ("gate", "up", "down"):
                shared[f"ffn_{ab}_w_{n}{l}"] = np.ascontiguousarray(inp[f"ffn_{ab}_w_{n}"][l])
    for n in ("even_w_in", "even_w_uq", "even_w_ukv", "even_w_out", "odd_w_in", "odd_w_out"):
        shared[n] = np.ascontiguousarray(inp[n][0])
    shared["odd_v_norm_g"] = np.ascontiguousarray(inp["odd_v_norm_g"][0]).reshape(1, D)
    shared["odd_v_norm_b"] = np.ascontiguousarray(inp["odd_v_norm_b"][0]).reshape(1, D)
    shared["odd_w_s"] = np.ascontiguousarray(inp["odd_w_s"][0])
    shared["odd_b_s"] = np.ascontiguousarray(inp["odd_b_s"][0]).reshape(1, 8 * 128)
    x = np.asarray(inp["x"])
    pos = np.asarray(inp["positions"])
    maps = []
    for c in range(n_cores):
        m = dict(shared)
        m["x"] = np.ascontiguousarray(x[2 * c:2 * c + 2].reshape(NTOK, D))
        m["pos"] = np.ascontiguousarray(pos[2 * c:2 * c + 2].reshape(1, NTOK)).astype(np.int32)
        maps.append(m)
    return maps


_CACHE = {}


def kernel(**inputs):
    inputs = {k: np.asarray(v) for k, v in inputs.items()}
    if "nc" not in _CACHE:
        _CACHE["nc"] = build_program(ALL_STAGES, list(range(NTOK // T)))[0]
    nc = _CACHE["nc"]
    maps = make_in_maps(inputs)
    res = run_bass_kernel_spmd(nc, maps, core_ids=list(range(N_CORES)))
    out = np.stack([r["y"].reshape(2, SEQ, D) for r in res.results], axis=0).reshape(16, SEQ, D)
    return out.astype(np.float32, copy=False)
```
